# Optimizing a Trainium2 kernel written in Bass

```python
import jax, jax.numpy as jnp
from jax import lax
import numpy as np

D_MODEL = 2048
BATCH = 1
SEQ = 8192
DEPTH = 2
DEC_BATCH = 4
DEC_SEQ = 8192
PAST_LEN = 128

MIX_WIDTH = D_MODEL
HEAD_DIM = 128
ATTN_WIDTH = MIX_WIDTH // 2
N_Q_HEADS = ATTN_WIDTH // HEAD_DIM
N_KV_HEADS = 2
Q_PER_KV = N_Q_HEADS // N_KV_HEADS
KV_WIDTH = N_KV_HEADS * HEAD_DIM
FOURIER_WIDTH = MIX_WIDTH - ATTN_WIDTH
N_FOURIER_GROUPS = 8
FOURIER_GROUP_DIM = FOURIER_WIDTH // N_FOURIER_GROUPS
IN_PROJ_WIDTH = ATTN_WIDTH + 2 * KV_WIDTH + FOURIER_WIDTH
WINDOW = 128
BLOCK = 128
ROPE_THETA = 10000.0
D_FF = 5632
N_SUBLAYERS = 3
N_MOD = 3
RMS_EPS = 1e-6
MOD_SCALE = 0.1
NEG_INF = -1e30

kernel_name = "hymba_style_fnet_swa_macaron_encoder"


def rms_norm(x, g):
    xf = x.astype(jnp.float32)
    y = xf * lax.rsqrt(jnp.mean(xf * xf, axis=-1, keepdims=True) + RMS_EPS)
    return (y * g.astype(jnp.float32)).astype(x.dtype)


def rope_tables(seq_len):
    inv_freq = ROPE_THETA ** (-jnp.arange(0, HEAD_DIM, 2, dtype=jnp.float32) / HEAD_DIM)
    ang = jnp.arange(seq_len, dtype=jnp.float32)[:, None] * inv_freq[None, :]
    return jnp.cos(ang), jnp.sin(ang)


def apply_rope(t, cos, sin):
    tf = t.astype(jnp.float32)
    t1, t2 = tf[..., : HEAD_DIM // 2], tf[..., HEAD_DIM // 2:]
    c = cos[None, :, None, :]
    s = sin[None, :, None, :]
    return jnp.concatenate([t1 * c - t2 * s, t1 * s + t2 * c], axis=-1).astype(t.dtype)


def band_mask(seq_len):
    nb = seq_len // BLOCK
    n = jnp.arange(nb)[:, None, None]
    i = jnp.arange(BLOCK)[None, :, None]
    j = jnp.arange(3 * BLOCK)[None, None, :]
    qpos = n * BLOCK + i
    kpos = (n - 1) * BLOCK + j
    return (jnp.abs(qpos - kpos) <= WINDOW) & (kpos >= 0) & (kpos < seq_len)


def windowed_gqa_attention(q, k, v, sink, cos, sin, mask):
    B, S, _ = q.shape
    nb = S // BLOCK
    q = apply_rope(q.reshape(B, S, N_Q_HEADS, HEAD_DIM), cos, sin)
    k = apply_rope(k.reshape(B, S, N_KV_HEADS, HEAD_DIM), cos, sin)
    v = v.reshape(B, S, N_KV_HEADS, HEAD_DIM)
    qb = q.reshape(B, nb, BLOCK, N_KV_HEADS, Q_PER_KV, HEAD_DIM)

    def band(t):
        tp = jnp.pad(t, ((0, 0), (BLOCK, BLOCK), (0, 0), (0, 0)))
        tp = tp.reshape(B, nb + 2, BLOCK, N_KV_HEADS, HEAD_DIM)
        return jnp.concatenate([tp[:, :-2], tp[:, 1:-1], tp[:, 2:]], axis=2)

    kb, vb = band(k), band(v)
    scores = jnp.einsum('bnqhgd,bnkhd->bnhgqk', qb, kb,
                        preferred_element_type=jnp.float32) * (HEAD_DIM ** -0.5)
    scores = jnp.where(mask[None, :, None, None], scores, NEG_INF)
    sink_l = sink.astype(jnp.float32).reshape(1, 1, N_KV_HEADS, Q_PER_KV, 1, 1)
    m = jnp.maximum(jnp.max(scores, axis=-1, keepdims=True), sink_l)
    p = jnp.exp(scores - m)
    probs = p / (jnp.sum(p, axis=-1, keepdims=True) + jnp.exp(sink_l - m))
    out = jnp.einsum('bnhgqk,bnkhd->bnqhgd', probs.astype(v.dtype), vb)
    return out.reshape(B, S, ATTN_WIDTH)


def fourier_mix(u, w_lin):
    B, S, _ = u.shape
    ug = u.reshape(B, S, N_FOURIER_GROUPS, FOURIER_GROUP_DIM).astype(jnp.float32)
    f = jnp.fft.fft2(ug, axes=(1, 3), norm='ortho').real.astype(u.dtype)
    out = jnp.einsum('bsgc,gce->bsge', f, w_lin)
    return out.reshape(B, S, FOURIER_WIDTH)


def swiglu(h, w_gate, w_up, w_down):
    return (jax.nn.silu(h @ w_gate) * (h @ w_up)) @ w_down


def encoder_trunk(x, c, w_mod, b_mod, pre_g, post_g, ffn_w_gate, ffn_w_up, ffn_w_down,
                  w_in, attn_sink, fourier_w, branch_g, w_out):
    B, S, D = x.shape
    cos, sin = rope_tables(S)
    mask = band_mask(S)
    c_act = jax.nn.silu(c)
    for l in range(DEPTH):
        mod = (c_act @ w_mod[l] + b_mod[l]).reshape(B, N_SUBLAYERS, N_MOD, D)

        def pre(xx, j):
            shift = mod[:, j, 0][:, None, :]
            scale = mod[:, j, 1][:, None, :]
            return rms_norm(xx, pre_g[l, j]) * (1.0 + scale) + shift

        def post(xx, y, j, weight):
            gate = mod[:, j, 2][:, None, :]
            return xx + weight * (1.0 + gate) * rms_norm(y, post_g[l, j])

        h = pre(x, 0)
        x = post(x, swiglu(h, ffn_w_gate[l, 0], ffn_w_up[l, 0], ffn_w_down[l, 0]), 0, 0.5)

        h = pre(x, 1)
        proj = h @ w_in[l]
        q = proj[..., :ATTN_WIDTH]
        k = proj[..., ATTN_WIDTH:ATTN_WIDTH + KV_WIDTH]
        v = proj[..., ATTN_WIDTH + KV_WIDTH:ATTN_WIDTH + 2 * KV_WIDTH]
        u = proj[..., ATTN_WIDTH + 2 * KV_WIDTH:]
        a_out = rms_norm(windowed_gqa_attention(q, k, v, attn_sink[l], cos, sin, mask), branch_g[l, 0])
        f_out = rms_norm(fourier_mix(u, fourier_w[l]), branch_g[l, 1])
        y = jnp.concatenate([a_out, f_out], axis=-1) @ w_out[l]
        x = post(x, y, 1, 1.0)

        h = pre(x, 2)
        x = post(x, swiglu(h, ffn_w_gate[l, 1], ffn_w_up[l, 1], ffn_w_down[l, 1]), 2, 0.5)
    return x


def setup_inputs(seed: int = 0) -> dict:
    key = jax.random.key(seed)
    ks = jax.random.split(key, 18)
    f32 = jnp.float32

    def nrm(k, shape, scale):
        return jax.random.normal(k, shape, f32) * scale

    return {
        "x_prompt": nrm(ks[0], (BATCH, SEQ, D_MODEL), 1.0),
        "x_sample": nrm(ks[1], (DEC_BATCH, DEC_SEQ, D_MODEL), 1.0),
        "c_prompt": nrm(ks[2], (BATCH, D_MODEL), 1.0),
        "c_sample": nrm(ks[3], (DEC_BATCH, D_MODEL), 1.0),
        "w_mod": nrm(ks[4], (DEPTH, D_MODEL, N_SUBLAYERS * N_MOD * D_MODEL), MOD_SCALE * D_MODEL ** -0.5),
        "b_mod": nrm(ks[5], (DEPTH, N_SUBLAYERS * N_MOD * D_MODEL), 0.01),
        "pre_g": 1.0 + nrm(ks[6], (DEPTH, N_SUBLAYERS, D_MODEL), 0.05),
        "post_g": 1.0 + nrm(ks[7], (DEPTH, N_SUBLAYERS, D_MODEL), 0.05),
        "ffn_w_gate": nrm(ks[8], (DEPTH, 2, D_MODEL, D_FF), D_MODEL ** -0.5),
        "ffn_w_up": nrm(ks[9], (DEPTH, 2, D_MODEL, D_FF), D_MODEL ** -0.5),
        "ffn_w_down": nrm(ks[10], (DEPTH, 2, D_FF, D_MODEL), D_FF ** -0.5),
        "w_in": nrm(ks[11], (DEPTH, D_MODEL, IN_PROJ_WIDTH), D_MODEL ** -0.5),
        "attn_sink": nrm(ks[12], (DEPTH, N_Q_HEADS), 0.5),
        "fourier_w": nrm(ks[13], (DEPTH, N_FOURIER_GROUPS, FOURIER_GROUP_DIM, FOURIER_GROUP_DIM), FOURIER_GROUP_DIM ** -0.5),
        "branch_g": 1.0 + nrm(ks[14], (DEPTH, 2, ATTN_WIDTH), 0.05),
        "w_out": nrm(ks[15], (DEPTH, MIX_WIDTH, D_MODEL), MIX_WIDTH ** -0.5),
    }


def reference(x_prompt, x_sample, c_prompt, c_sample, w_mod, b_mod, pre_g, post_g,
              ffn_w_gate, ffn_w_up, ffn_w_down, w_in, attn_sink, fourier_w, branch_g, w_out):
    y_prompt = encoder_trunk(x_prompt, c_prompt, w_mod, b_mod, pre_g, post_g, ffn_w_gate, ffn_w_up,
                             ffn_w_down, w_in, attn_sink, fourier_w, branch_g, w_out)
    y_sample = encoder_trunk(x_sample, c_sample, w_mod, b_mod, pre_g, post_g, ffn_w_gate, ffn_w_up,
                             ffn_w_down, w_in, attn_sink, fourier_w, branch_g, w_out)
    return (y_prompt, y_sample)
```

```python
import math
import os
import numpy as np
import concourse.bass as bass
import concourse.mybir as mybir
from concourse.bass_utils import run_bass_kernel_spmd

F32 = mybir.dt.float32
BF16 = mybir.dt.bfloat16
AF = mybir.ActivationFunctionType
ALU = mybir.AluOpType
AX = mybir.AxisListType

D = 2048
KC = 16
T = 512
HD = 128
NQH = 8
NKV = 2
NG = 8
EPS = 1e-6
NEG = -1e30
VROWS = 40


class Tok:
    __slots__ = ("sem", "val", "key")

    def __init__(self, sem, val, key):
        self.sem, self.val, self.key = sem, val, key


class Buf:
    __slots__ = ("name", "w", "rs", "dsem", "dcnt", "excl")

    def __init__(self, name, excl=False):
        self.name = name
        self.excl = excl
        self.w = {}
        self.rs = {}
        self.dsem = None
        self.dcnt = 0


class Eng:
    def __init__(self, prog, name):
        self.prog = prog
        self.name = name
        self.ops = []
        self.sem = prog.nc.alloc_semaphore(name="e_" + name)
        self.key = "e_" + name
        self.cnt = 0
        self.seen = {}
        self.pend_r, self.pend_w = [], []

    def wait(self, tok):
        if tok is None:
            return
        if self.name == "pe" and tok.key == self.key:
            return
        if self.seen.get(tok.key, 0) >= tok.val:
            return
        self.seen[tok.key] = tok.val
        sem, val = tok.sem, tok.val
        self.ops.append(lambda e: e.wait_ge(sem, val))

    def deps(self, reads, writes):
        for b in reads:
            for w_ in b.w.values():
                self.wait(w_)
            if b.excl:
                for r in b.rs.values():
                    self.wait(r)
        for b in writes:
            for w_ in b.w.values():
                self.wait(w_)
            for r in b.rs.values():
                self.wait(r)

    def record(self, tok, reads, writes):
        for b in reads:
            if b.excl:
                b.w[tok.key] = tok
                b.rs = {}
                continue
            o = b.rs.get(tok.key)
            if o is None or o.val < tok.val:
                b.rs[tok.key] = tok
        for b in writes:
            b.w[tok.key] = tok
            b.rs = {}

    def op(self, fn, reads=(), writes=(), signal=True):
        self.deps(reads, writes)
        if not signal:
            self.ops.append(fn)
            self.pend_r.extend(reads)
            self.pend_w.extend(writes)
            return None
        self.cnt += 1
        sem = self.sem
        self.ops.append(lambda e: fn(e).then_inc(sem, 1))
        tok = Tok(sem, self.cnt, self.key)
        self.record(tok, list(reads) + self.pend_r, list(writes) + self.pend_w)
        self.pend_r, self.pend_w = [], []
        self.prog.last[self.key] = tok
        return tok

    def dma(self, out_ap, in_ap, reads=(), writes=(), sembuf=None, track_last=True, **kw):
        self.deps(reads, writes)
        sb = sembuf
        if sb.dsem is None:
            sb.dsem = self.prog.nc.alloc_semaphore(name="d_" + sb.name)
        key = "d_" + sb.name
        if sb.dcnt > 0:
            self.wait(Tok(sb.dsem, sb.dcnt, key))
        sb.dcnt += 16
        sem, val = sb.dsem, sb.dcnt
        self.ops.append(lambda e: e.dma_start(out=out_ap, in_=in_ap, **kw).then_inc(sem, 16))
        tok = Tok(sem, val, key)
        self.record(tok, reads, writes)
        if track_last:
            self.prog.last[key] = tok
        return tok


class Prog:
    def __init__(self):
        self.nc = bass.Bass("TRN2", target_bir_lowering=False)
        self.last = {}
        self.pe = Eng(self, "pe")
        self.act = Eng(self, "act")
        self.dve = Eng(self, "dve")
        self.pool = Eng(self, "pool")
        self.sp = Eng(self, "sp")
        self.engs = [self.pe, self.act, self.dve, self.pool, self.sp]

    def mm(self, out_ap, out_buf, lhsT, rhs, reads, start, stop, signal=None, **kw):
        pe = self.pe
        if signal is None:
            signal = stop
        if start:
            pe.deps(reads, [out_buf])
        else:
            pe.deps(reads, [])
        if signal:
            pe.cnt += 1
            sem = pe.sem
            pe.ops.append(lambda e: e.matmul(out_ap, lhsT, rhs, start=start, stop=stop, **kw).then_inc(sem, 1))
            tok = Tok(sem, pe.cnt, pe.key)
            pe.record(tok, list(reads) + pe.pend_r, [out_buf] + pe.pend_w)
            pe.pend_r, pe.pend_w = [], []
            self.last[pe.key] = tok
            return tok
        pe.ops.append(lambda e: e.matmul(out_ap, lhsT, rhs, start=start, stop=stop, **kw))
        pe.pend_r.extend(reads)
        pe.pend_w.append(out_buf)
        return None

    def mm_group(self, out_ap, out_buf, items, reads, item_reads=None):
        pe = self.pe
        pe.deps(reads, [out_buf])
        n = len(items)
        if item_reads is not None:
            reads = list(reads) + list(item_reads)
        for i, (l, r) in enumerate(items):
            st, last = (i == 0), (i == n - 1)
            if item_reads is not None:
                pe.deps([item_reads[i]], [])
            if last:
                pe.cnt += 1
                sem = pe.sem
                pe.ops.append(lambda e, l=l, r=r, st=st: e.matmul(out_ap, l, r, start=st, stop=True).then_inc(sem, 1))
            else:
                pe.ops.append(lambda e, l=l, r=r, st=st: e.matmul(out_ap, l, r, start=st, stop=False))
        tok = Tok(pe.sem, pe.cnt, pe.key)
        pe.record(tok, list(reads) + pe.pend_r, [out_buf] + pe.pend_w)
        pe.pend_r, pe.pend_w = [], []
        self.last[pe.key] = tok
        return tok

    def barrier(self):
        toks = list(self.last.values())
        for e in self.engs:
            for t in toks:
                e.wait(t)

    def finalize(self):
        nc = self.nc
        for t in list(self.last.values()):
            self.sp.wait(t)
        with nc.allow_non_contiguous_dma(reason="small vector / layout loads"):
            with nc.Block() as block:
                def mk(engw):
                    def body(e):
                        for f in engw.ops:
                            f(e)
                    return body
                block.tensor(mk(self.pe))
                block.scalar(mk(self.act))
                block.vector(mk(self.dve))
                block.gpsimd(mk(self.pool))
                block.sync(mk(self.sp))
        return nc


class Cfg:
    def __init__(self, S=8192, DFF=5632, L=2):
        self.S, self.DFF, self.L = S, DFF, L
        self.NT = S // T
        self.NJ = DFF // 128
        self.NA = S // 128
        self.NBLK = S // 128


def host_constants(cfg):
    S, NA = cfg.S, cfg.NA
    inv_freq = (10000.0 ** (-np.arange(0, HD, 2, dtype=np.float32) / HD)).astype(np.float32)
    ang = np.arange(S, dtype=np.float32)[:, None] * inv_freq[None, :]
    cos, sin = np.cos(ang).astype(np.float32), np.sin(ang).astype(np.float32)
    ropeC = np.concatenate([cos, cos], axis=1).T.copy()
    ropeS = np.concatenate([sin, sin], axis=1).T.copy()
    pm = np.zeros((128, 128), np.float32)
    for m in range(64):
        pm[m + 64, m] = -1.0
    for m in range(64, 128):
        pm[m - 64, m] = 1.0
    i = np.arange(128)[:, None]
    j = np.arange(384)[None, :]
    mask = np.where((j >= i) & (j <= i + 256), 0.0, NEG).astype(np.float32)
    c = np.arange(128)
    scale = 1.0 / math.sqrt(S * 128.0)
    angc = 2.0 * np.pi * ((c[:, None] * c[None, :]) % 128) / 128.0
    ccs = (np.cos(angc) * scale).astype(np.float32)
    scs = (-np.sin(angc) * scale).astype(np.float32)
    a = np.arange(NA)
    anga = 2.0 * np.pi * ((a[:, None] * a[None, :]) % NA) / NA
    ca, sa = np.cos(anga), np.sin(anga)
    r1 = np.concatenate([ca, -sa], axis=1).astype(np.float32)
    r2 = np.concatenate([sa, ca], axis=1).astype(np.float32)
    b = np.arange(128)
    sp = np.arange(S)
    angt = 2.0 * np.pi * ((b[:, None].astype(np.int64) * sp[None, :].astype(np.int64)) % S) / S
    tc = np.cos(angt).reshape(128, 128, NA).transpose(0, 2, 1)
    ts = np.sin(angt).reshape(128, 128, NA).transpose(0, 2, 1)
    return {
        "ropeC": ropeC, "ropeS": ropeS, "pmat": pm, "ident": np.eye(128, dtype=np.float32),
        "ones": np.ones((128, 128), np.float32), "maskb": mask, "ccs": ccs, "scs": scs,
        "r1": r1, "r2": r2,
        "tcp": np.ascontiguousarray(tc).astype(np.float32).reshape(128, NA * 128),
        "tsp": np.ascontiguousarray(ts).astype(np.float32).reshape(128, NA * 128),
    }


def build(cfg):
    S, DFF, L, NT, NJ, NA = cfg.S, cfg.DFF, cfg.L, cfg.NT, cfg.NJ, cfg.NA
    P = Prog()
    nc = P.nc
    pe, act, dve, pool, sp = P.pe, P.act, P.dve, P.pool, P.sp

    def din(name, shape):
        return nc.dram_tensor(name, list(shape), F32, kind="ExternalInput").ap()

    x_d = din("x", [S, D])
    c_d = din("c", [1, D])
    wmod_d = din("w_mod", [L, D, 9 * D])
    bmod_d = din("b_mod", [L, 9 * D])
    preg_d = din("pre_g", [L, 3, D])
    postg_d = din("post_g", [L, 3, D])
    wg_d = din("ffn_w_gate", [L, 2, D, DFF])
    wu_d = din("ffn_w_up", [L, 2, D, DFF])
    wd_d = din("ffn_w_down", [L, 2, DFF, D])
    win_d = din("w_in", [L, D, 2560])
    sink_d = din("attn_sink", [L, 8])
    fw_d = din("fourier_w", [L, 8, 128, 128])
    bg_d = din("branch_g", [L, 2, 1024])
    wo_d = din("w_out", [L, D, D])
    ropeC_d = din("ropeC", [128, S])
    ropeS_d = din("ropeS", [128, S])
    pmat_d = din("pmat", [128, 128])
    ident_d = din("ident", [128, 128])
    ones_d = din("ones", [128, 128])
    maskb_d = din("maskb", [128, 384])
    ccs_d = din("ccs", [128, 128])
    scs_d = din("scs", [128, 128])
    r1_d = din("r1", [NA, 2 * NA])
    r2_d = din("r2", [NA, 2 * NA])
    tcp_d = din("tcp", [128, NA * 128])
    tsp_d = din("tsp", [128, NA * 128])
    y_d = nc.dram_tensor("y", [S, D], F32, kind="ExternalOutput").ap()

    DBG = bool(int(os.environ.get("KDBG", "0")))

    def dscr(name, shape, dt=BF16):
        return nc.dram_tensor(name, list(shape), dt, kind=("ExternalOutput" if DBG and name in ("QS", "KS", "VS", "US", "FS", "XS") else "Internal")).ap()

    Wgu = dscr("Wgu", [L, 2, NJ, 128, 2, KC, 128])
    Wd = dscr("Wd", [L, 2, 16, 128, NJ, 128])
    Win = dscr("Win", [L, 18, 128, KC, 128])
    Wv = dscr("Wv", [L, 128, KC, 256])
    Wo = dscr("Wo", [L, 16, 128, KC, 128])
    QS = dscr("QS", [2, 8, 128, S])
    KS = dscr("KS", [2, 2, 128, S])
    VS = dscr("VS", [2, S, 256])
    US = dscr("US", [2, 8, 128, S])
    FS = dscr("FS", [2, 8, 128, S])
    XS = dscr("XS", [KC, 128, S], F32)

    def sb(name, shape, dt):
        return nc.alloc_sbuf_tensor("s_" + name, list(shape), dt).ap()

    ident32 = sb("ident32", [128, 128], F32)
    pmat = sb("pmat", [128, 128], F32)
    ccs = sb("ccs", [128, 128], F32)
    scs = sb("scs", [128, 128], F32)
    identb = sb("identb", [128, 128], BF16)
    onesb = sb("onesb", [128, 128], BF16)
    maskb = sb("maskb", [128, 384], BF16)
    r1 = sb("r1", [NA, 2 * NA], BF16)
    r2 = sb("r2", [NA, 2 * NA], BF16)
    epst = sb("epst", [128, 1], F32)
    vecT = sb("vecT", [128, KC, VROWS], F32)
    cact = sb("cact", [128, KC, 2], F32)
    modT = sb("modT", [128, L, 9, KC], F32)
    gsT = sb("gsT", [128, L, 3, KC], F32)
    coefT = sb("coefT", [128, L, 3, KC], F32)
    sinkb = sb("sinkb", [128, L, 8], F32)
    nsinkb = sb("nsinkb", [128, L, 8], F32)
    small = sb("small", [128, 64], F32)
    diag2 = sb("diag2", [128, 8, 128], BF16)
    b_const = Buf("const")
    b_small = [Buf("sm%d" % i) for i in range(8)]
    b_diag2 = [Buf("diag%d" % i) for i in range(8)]
    b_sm2 = [[Buf("smx%d_%d" % (p_, i)) for i in range(7)] for p_ in range(2)]
    b_ps = [Buf("pA"), Buf("pB")]

    XT_B, HT_B, ACT_B, YT_B = 32768, 16384, NJ * 1024 if NJ * 1024 > 45056 else 45056, 32768
    WSLOT = max(2 * KC * 128 * 2, NJ * 128 * 2, KC * 256 * 2)
    NW = 3
    SCR_B = 4 * 1024 + 4 * 2048 + 2048 + 4096
    tot = XT_B + HT_B + ACT_B + YT_B + NW * WSLOT + SCR_B
    arena = nc.alloc_sbuf_tensor("arena", [128, tot // 2], BF16).ap()
    off = [0]

    def carve(nbytes):
        a = arena[:, off[0] // 2:(off[0] + nbytes) // 2]
        off[0] += nbytes
        return a

    xT = carve(XT_B).bitcast(F32).rearrange("p (c t) -> p c t", c=KC)
    hT = carve(HT_B).rearrange("p (c t) -> p c t", c=KC)
    act_raw = carve(ACT_B)
    yT_raw = carve(YT_B)
    yT = yT_raw.bitcast(F32).rearrange("p (c t) -> p c t", c=KC)
    wslots = [carve(WSLOT) for _ in range(NW)]
    sqs = [carve(1024) for _ in range(4)]
    tmps = [carve(2048).bitcast(F32) for _ in range(4)]
    rstd_t = carve(2048).bitcast(F32)
    sd_t = rstd_t
    ropeCt = carve(2048).bitcast(F32)
    ropeSt = carve(2048).bitcast(F32)
    main_bytes = off[0]

    actc = act_raw[:, 0:NJ * 512].rearrange("p (j t) -> p j t", j=NJ)
    qv = act_raw[:, 0:8 * 512].rearrange("p (h t) -> p h t", h=8)
    kv = act_raw[:, 8 * 512:8 * 512 + 2 * 768].rearrange("p (g t) -> p g t", g=2)
    vv = act_raw[:, 11 * 512:11 * 512 + 6 * 256].rearrange("p (b c) -> p b c", b=6)
    pv = act_raw[:, 14 * 512:14 * 512 + 4 * 384].rearrange("p (h k) -> p h k", h=4)
    ptv = act_raw[:, 17 * 512:17 * 512 + 3 * 512].rearrange("p (k t) -> p k t", k=3)
    pvs = [pv, yT_raw[:, 0:4 * 384].rearrange("p (h k) -> p h k", h=4)]
    aoT = act_raw[:, 20 * 512:36 * 512].bitcast(F32).rearrange("p (h t) -> p h t", h=8)
    ftv = act_raw[:, 36 * 512:44 * 512].rearrange("p (g t) -> p g t", g=8)
    qst = act_raw[:, 0:10 * 512].rearrange("p (h t) -> p h t", h=10)
    ust = act_raw[:, 10 * 512:18 * 512].rearrange("p (g t) -> p g t", g=8)
    vst = act_raw[:, 18 * 512:18 * 512 + 4 * 256].rearrange("p (b c) -> p b c", b=4)
    qf_t = act_raw[:, 20 * 512:20 * 512 + 2 * 1024].bitcast(F32).rearrange("p (k t) -> p k t", k=2)
    xtok = yT_raw.bitcast(F32).rearrange("p (b d) -> p b d", b=4)

    PS = nc.alloc_psum_tensor("PS", [128, 8 * 512], F32).ap()
    bank = [PS[:, i * 512:(i + 1) * 512] for i in range(8)]
    b_bank = [Buf("bank%d" % i, excl=True) for i in range(8)]

    b_xT = [Buf("xT%d" % i) for i in range(KC)]
    b_hT = [Buf("hT%d" % i) for i in range(KC)]
    b_yT = [Buf("yT%d" % i) for i in range(KC)]
    b_act = [Buf("act%d" % i) for i in range(max(NJ, 44))]
    b_w = [Buf("w%d" % i) for i in range(NW)]
    b_sq = [Buf("sq%d" % i) for i in range(4)]
    b_tmp = [Buf("tmp%d" % i) for i in range(4)]
    b_rstd = Buf("rstd")
    b_sd = Buf("sd")
    b_rope = Buf("rope")
    b_q, b_k, b_v, b_ft = Buf("q"), Buf("k"), Buf("v"), Buf("ft")
    b_p = Buf("p")
    b_pt = [Buf("pt%d" % i) for i in range(3)]
    b_ao = [Buf("ao%d" % i) for i in range(8)]
    b_qst = [Buf("qst%d" % i) for i in range(10)]
    b_ust = [Buf("ust%d" % i) for i in range(8)]
    b_vst = Buf("vst")
    b_qf = [Buf("qf%d" % i) for i in range(2)]
    b_xtok = Buf("xtok")
    b_dram = Buf("dram")

    cnt = {"w": 0, "ev": 0}
    bufcache = {}

    def getbuf(name):
        if name not in bufcache:
            bufcache[name] = Buf(name)
        return bufcache[name]

    conv_toks = {}
    pool_free = [False]

    def wload(src_ap, view_fn, grp):
        s = cnt["w"] % NW
        cnt["w"] += 1
        v = view_fn(wslots[s])
        for t_ in conv_toks.get(grp, ()):
            sp.wait(t_)
        sp.dma(v, src_ap, writes=[b_w[s]], sembuf=b_w[s])
        return v, b_w[s]

    def evac_engine():
        cnt["ev"] += 1
        return act if cnt["ev"] % 2 == 0 else dve

    def copy_op(eng, out, in_, reads, writes):
        if eng is act:
            return act.op(lambda e: e.activation(out=out, in_=in_, func=AF.Copy), reads, writes)
        return eng.op(lambda e: e.tensor_copy(out=out, in_=in_), reads, writes)

    def prologue():
        k = 0
        for dst, src in ((ident32, ident_d), (pmat, pmat_d), (ccs, ccs_d), (scs, scs_d)):
            sp.dma(dst, src, writes=[b_const], sembuf=b_small[k % 8]); k += 1
        for dst, src in ((identb, ident_d), (onesb, ones_d), (maskb, maskb_d), (r1, r1_d), (r2, r2_d)):
            pool.dma(dst, src, writes=[b_const], sembuf=b_small[k % 8]); k += 1
        sp.dma(sinkb.rearrange("p l h -> p (l h)"),
               sink_d.rearrange("(o l) h -> o (l h)", o=1).broadcast_to([128, L * 8]),
               writes=[b_const], sembuf=b_small[k % 8]); k += 1
        V = yT_raw.bitcast(F32)[0:VROWS, 0:D]
        pool.op(lambda e: e.memset(yT_raw.bitcast(F32)[0:64, 0:D], 0.0), writes=[b_xtok])
        pool.op(lambda e: e.memset(epst, EPS), writes=[b_const])
        if STOP >= 2:
            convert_weights()
        r = 0
        rows = {}
        for name, src, n in (("pre", preg_d.rearrange("l j d -> (l j) d"), 3 * L),
                             ("post", postg_d.rearrange("l j d -> (l j) d"), 3 * L),
                             ("bmod", bmod_d.rearrange("l (m d) -> (l m) d", d=D), 9 * L),
                             ("c", c_d, 1)):
            rows[name] = r
            sp.dma(V[r:r + n, :], src, writes=[b_xtok], sembuf=b_small[k % 8]); k += 1
            r += n
        rows["bg"] = r
        sp.dma(V[r:r + 2 * L, 0:1024], bg_d.rearrange("l b d -> (l b) d"), writes=[b_xtok], sembuf=b_small[k % 8]); k += 1
        r += 2 * L
        assert r <= VROWS
        for half in range(2):
            for cc in range(8):
                c = half * 8 + cc
                o = bank[half][:, cc * VROWS:(cc + 1) * VROWS]
                i_ = V[:, c * 128:(c + 1) * 128]
                pe.op(lambda e, o=o, i_=i_: e.transpose(o, i_, ident32[0:VROWS, 0:VROWS]),
                      reads=[b_xtok, b_const], writes=[b_bank[half]])
            o = vecT[:, half * 8:(half + 1) * 8, :]
            i_ = bank[half][:, 0:8 * VROWS].rearrange("p (c r) -> p c r", c=8)
            dve.op(lambda e, o=o, i_=i_: e.tensor_copy(out=o, in_=i_), reads=[b_bank[half]], writes=[b_const])
        for dup in range(2):
            o = cact[:, :, dup]
            i_ = vecT[:, :, rows["c"]]
            act.op(lambda e, o=o, i_=i_: e.activation(out=o, in_=i_, func=AF.Silu), reads=[b_const], writes=[b_const])
        dve.op(lambda e: e.tensor_scalar(out=nsinkb, in0=sinkb, scalar1=-1.0, scalar2=None, op0=ALU.mult),
               reads=[b_const], writes=[b_const])
        OCB = 4
        stg = [arena[:, 0:KC * OCB * 128 * 2].bitcast(F32).rearrange("p (k c) -> p k c", k=KC),
               act_raw[:, 0:KC * OCB * 128 * 2].bitcast(F32).rearrange("p (k c) -> p k c", k=KC)]
        b_stg = [b_act[0], b_act[1]]
        nblk = 144 // OCB
        for l in range(L):
            mb = bank[2 + (l % 2)]
            for blk in range(nblk):
                s = (l * nblk + blk) % 2
                src = wmod_d[l, :, blk * OCB * 128:(blk + 1) * OCB * 128].rearrange("(k p) c -> p k c", p=128)
                sp.dma(stg[s], src, writes=[b_stg[s]], sembuf=b_stg[s])
                for oc in range(OCB):
                    col = (blk * OCB + oc) * 2
                    for kc in range(KC):
                        P.mm(mb[:, col:col + 2], b_bank[2 + (l % 2)], stg[s][:, kc, oc * 128:(oc + 1) * 128], cact[:, kc, :],
                             reads=[b_stg[s], b_const], start=(kc == 0), stop=(kc == KC - 1),
                             signal=(kc == KC - 1 and oc == OCB - 1))
            i0 = mb[:, 0:288].rearrange("p (jm c two) -> p jm c two", jm=9, c=KC)[:, :, :, 0]
            i1 = vecT[:, :, rows["bmod"] + l * 9:rows["bmod"] + (l + 1) * 9].rearrange("p c jm -> p jm c")
            o = modT[:, l]
            dve.op(lambda e, o=o, i0=i0, i1=i1: e.tensor_tensor(out=o, in0=i0, in1=i1, op=ALU.add),
                   reads=[b_bank[2 + (l % 2)], b_const], writes=[b_const])
            for j in range(3):
                wgt = (0.5, 1.0, 0.5)[j]
                o = gsT[:, l, j, :]
                sc = modT[:, l, 3 * j + 1, :]
                pg = vecT[:, :, rows["pre"] + l * 3 + j]
                dve.op(lambda e, o=o, sc=sc, pg=pg: e.scalar_tensor_tensor(out=o, in0=sc, scalar=1.0, in1=pg, op0=ALU.add, op1=ALU.mult),
                       reads=[b_const], writes=[b_const])
                o2 = coefT[:, l, j, :]
                gt = modT[:, l, 3 * j + 2, :]
                qg = vecT[:, :, rows["post"] + l * 3 + j]
                dve.op(lambda e, o2=o2, gt=gt, qg=qg: e.scalar_tensor_tensor(out=o2, in0=gt, scalar=1.0, in1=qg, op0=ALU.add, op1=ALU.mult),
                       reads=[b_const], writes=[b_const])
                if wgt != 1.0:
                    dve.op(lambda e, o2=o2, wgt=wgt: e.tensor_scalar(out=o2, in0=o2, scalar1=wgt, scalar2=None, op0=ALU.mult),
                           reads=[b_const], writes=[b_const])
        return rows

    def convert_weights():
        cvb = [Buf("cv%d" % i) for i in range(6)]
        k = [0]
        cur = [None]

        def cv(out_ap, in_ap):
            t_ = pool.dma(out_ap, in_ap, sembuf=cvb[k[0] % 6], track_last=False, max_dma_last_dim=4096)
            conv_toks.setdefault(cur[0], {})[t_.key] = t_
            k[0] += 1

        def conv_ffn(l, f):
            cur[0] = ("ffn", l, f)
            for m, src in ((0, wg_d), (1, wu_d)):
                v = src[l, f].rearrange("(kc p) (j c) -> kc j p c", p=128, c=128)
                for kc in range(KC):
                    cv(Wgu[l, f, :, :, m, kc, :], v[kc])
            v = wd_d[l, f].rearrange("(jc p) (i c) -> jc i p c", p=128, c=128)
            for jc in range(NJ):
                cv(Wd[l, f, :, :, jc, :], v[jc])

        def conv_in(l):
            cur[0] = ("in", l)
            v = win_d[l].rearrange("(kc p) (j c) -> kc j p c", p=128, c=128)
            for kc in range(KC):
                cv(Win[l, 0:10, :, kc, :], v[kc, 0:10])
                cv(Win[l, 10:18, :, kc, :], v[kc, 12:20])
            v2 = win_d[l].rearrange("(kc p) c -> kc p c", p=128)
            for kc in range(KC):
                cv(Wv[l, :, kc, :], v2[kc, :, 1280:1536])

        def conv_out(l):
            cur[0] = ("out", l)
            v = wo_d[l].rearrange("(kc p) (j c) -> kc j p c", p=128, c=128)
            for kc in range(KC):
                cv(Wo[l, :, :, kc, :], v[kc])

        conv_ffn(0, 0)
        conv_in(0)
        for l in range(L):
            conv_out(l)
            conv_ffn(l, 1)
            if l + 1 < L:
                conv_ffn(l + 1, 0)
                conv_in(l + 1)
        for g_ in list(conv_toks):
            conv_toks[g_] = list(conv_toks[g_].values())

    def stats_rstd(srcs, dim):
        n = len(srcs)
        for i, (ap, bufs) in enumerate(srcs):
            s = i % 4
            if i % 3 == 2 and pool_free[0]:
                pool.op(lambda e, ap=ap, s=s: e.tensor_tensor(out=sqs[s], in0=ap, in1=ap, op=ALU.mult), reads=bufs, writes=[b_sq[s]])
            else:
                act.op(lambda e, ap=ap, s=s: e.activation(out=sqs[s], in_=ap, func=AF.Square), reads=bufs, writes=[b_sq[s]])
            P.mm(bank[6], b_bank[6], onesb, sqs[s], reads=[b_sq[s], b_const], start=(i == 0), stop=(i == n - 1), signal=True)
        finish_rstd(dim)

    def finish_rstd(dim):
        if int(os.environ.get("KRSQ", "0")):
            act.op(lambda e: e.activation(out=rstd_t, in_=bank[6], func=AF.Abs_reciprocal_sqrt, bias=epst, scale=1.0 / dim),
                   reads=[b_bank[6], b_const], writes=[b_rstd])
        else:
            act.op(lambda e: e.activation(out=rstd_t, in_=bank[6], func=AF.Sqrt, bias=epst, scale=1.0 / dim),
                   reads=[b_bank[6], b_const], writes=[b_rstd])
            dve.op(lambda e: e.reciprocal(out=rstd_t, in_=rstd_t), reads=[], writes=[b_rstd])

    def prenorm(l, j):
        stats_rstd([(xT[:, c, :], [b_xT[c]]) for c in range(KC)], D)
        for c in range(KC):
            s = c % 4
            eng = dve if (c % 3 != 2 or not pool_free[0]) else pool
            eng.op(lambda e, c=c, s=s: e.tensor_tensor(out=tmps[s], in0=xT[:, c, :], in1=rstd_t, op=ALU.mult),
                   reads=[b_xT[c], b_rstd], writes=[b_tmp[s]])
            act.op(lambda e, c=c, s=s: e.activation(out=hT[:, c, :], in_=tmps[s], func=AF.Identity,
                                                    scale=gsT[:, l, j, c:c + 1], bias=modT[:, l, 3 * j, c:c + 1]),
                   reads=[b_tmp[s], b_const], writes=[b_hT[c]])

    def postnorm(l, j):
        flush_stats()
        finish_rstd(D)
        for c in range(KC):
            s = c % 4
            act.op(lambda e, c=c: e.activation(out=yT[:, c, :], in_=yT[:, c, :], func=AF.Identity, scale=coefT[:, l, j, c:c + 1]),
                   reads=[b_const], writes=[b_yT[c]])
            e1 = pool if (pool_free[0] and c % 3 == 2) else dve
            e2 = pool if (pool_free[0] and c % 3 == 0) else dve
            e1.op(lambda e, c=c, s=s: e.tensor_tensor(out=tmps[s], in0=yT[:, c, :], in1=rstd_t, op=ALU.mult),
                  reads=[b_yT[c], b_rstd], writes=[b_tmp[s]])
            e2.op(lambda e, c=c, s=s: e.tensor_tensor(out=xT[:, c, :], in0=xT[:, c, :], in1=tmps[s], op=ALU.add),
                  reads=[b_tmp[s]], writes=[b_xT[c]])

    pend_stats = []

    def flush_stats():
        while pend_stats:
            s_, i_ = pend_stats.pop(0)
            P.mm(bank[6], b_bank[6], onesb, sqs[s_], reads=[b_sq[s_], b_const], start=(i_ == 0), stop=(i_ == KC - 1), signal=True)

    def evac_y(i, ob):
        s = i % 4
        flush_stats()
        dve.op(lambda e, i=i, ob=ob: e.tensor_copy(out=yT[:, i, :], in_=bank[ob]), reads=[b_bank[ob]], writes=[b_yT[i]])
        act.op(lambda e, s=s, i=i: e.activation(out=sqs[s], in_=yT[:, i, :], func=AF.Square), reads=[b_yT[i]], writes=[b_sq[s]])
        pend_stats.append((s, i))

    def ffn(l, f, j):
        FS_ = int(os.environ.get("KFFN", "9"))
        prenorm(l, j)
        if FS_ < 2:
            return
        for jj in range(NJ):
            w, bw = wload(Wgu[l, f, jj], lambda a: a[:, 0:2 * KC * 128].rearrange("p (m k c) -> p m k c", m=2, k=KC), ("ffn", l, f))
            st = jj % 2
            gb, ub = 2 * st, 2 * st + 1
            P.mm_group(bank[gb], b_bank[gb], [(w[:, 0, kc, :], hT[:, kc, :]) for kc in range(KC)], reads=[bw], item_reads=b_hT)
            P.mm_group(bank[ub], b_bank[ub], [(w[:, 1, kc, :], hT[:, kc, :]) for kc in range(KC)], reads=[bw], item_reads=b_hT)
            act.op(lambda e, st=st, gb=gb: e.activation(out=tmps[st], in_=bank[gb], func=AF.Silu), reads=[b_bank[gb]], writes=[b_tmp[st]])
            dve.op(lambda e, st=st, ub=ub, jj=jj: e.tensor_tensor(out=actc[:, jj, :], in0=tmps[st], in1=bank[ub], op=ALU.mult),
                   reads=[b_tmp[st], b_bank[ub]], writes=[b_act[jj]])
        if FS_ < 3:
            return
        for i in range(KC):
            w, bw = wload(Wd[l, f, i], lambda a: a[:, 0:NJ * 128].rearrange("p (j c) -> p j c", j=NJ), ("ffn", l, f))
            ob = 4 + (i % 2)
            P.mm_group(bank[ob], b_bank[ob], [(w[:, jc, :], actc[:, jc, :]) for jc in range(NJ)], reads=[bw], item_reads=b_act[:NJ])
            evac_y(i, ob)
        if FS_ < 4:
            return
        postnorm(l, j)

    def proj(l, t, nxt=None):
        t0 = t * T
        par = l % 2
        prenorm(l, 1)
        sp.dma(XS[:, :, t0:t0 + T].rearrange("c p t -> p c t"), xT, reads=b_xT, sembuf=b_xT[0])
        if nxt is not None:
            nxt()
        sp.dma(ropeCt, ropeC_d[:, t0:t0 + T], writes=[b_rope], sembuf=b_rope)
        sp.dma(ropeSt, ropeS_d[:, t0:t0 + T], writes=[b_rope], sembuf=b_small[0])
        for jq in range(18):
            w, bw = wload(Win[l, jq], lambda a: a[:, 0:KC * 128].rearrange("p (k c) -> p k c", k=KC), ("in", l))
            pb = jq % 4
            P.mm_group(bank[pb], b_bank[pb], [(w[:, kc, :], hT[:, kc, :]) for kc in range(KC)], reads=[bw], item_reads=b_hT)
            if jq < 10:
                s = jq % 2
                act.op(lambda e, s=s, pb=pb: e.activation(out=qf_t[:, s, :], in_=bank[pb], func=AF.Copy), reads=[b_bank[pb]], writes=[b_qf[s]])
                P.mm(bank[7], b_bank[7], pmat, qf_t[:, s, :], reads=[b_qf[s], b_const], start=True, stop=True)
                dve.op(lambda e, s=s: e.tensor_tensor(out=tmps[s], in0=bank[7], in1=ropeSt, op=ALU.mult),
                       reads=[b_bank[7], b_rope], writes=[b_tmp[s]])
                pe_ = pool if pool_free[0] else dve
                pe_.op(lambda e, s=s: e.tensor_tensor(out=qf_t[:, s, :], in0=qf_t[:, s, :], in1=ropeCt, op=ALU.mult),
                       reads=[b_rope], writes=[b_qf[s]])
                pe_.op(lambda e, s=s, jq=jq: e.tensor_tensor(out=qst[:, jq, :], in0=qf_t[:, s, :], in1=tmps[s], op=ALU.add),
                       reads=[b_qf[s], b_tmp[s]], writes=[b_qst[jq]])
            else:
                g = jq - 10
                copy_op(evac_engine(), ust[:, g, :], bank[pb], [b_bank[pb]], [b_ust[g]])
        sp.dma(QS[par, :, :, t0:t0 + T].rearrange("h d t -> d h t"), qst[:, 0:8, :], reads=b_qst[0:8], sembuf=b_qst[0])
        sp.dma(KS[par, :, :, t0:t0 + T].rearrange("h d t -> d h t"), qst[:, 8:10, :], reads=b_qst[8:10], sembuf=b_qst[8])
        sp.dma(US[par, :, :, t0:t0 + T].rearrange("g c t -> c g t"), ust, reads=b_ust, sembuf=b_ust[0])
        w, bw = wload(Wv[l], lambda a: a[:, 0:KC * 256].rearrange("p (k c) -> p k c", k=KC), ("in", l))
        for tb in range(4):
            pb = tb % 4
            P.mm_group(bank[pb][:, 0:256], b_bank[pb], [(hT[:, kc, tb * 128:(tb + 1) * 128], w[:, kc, :]) for kc in range(KC)],
                       reads=[bw] + b_hT)
            copy_op(evac_engine(), vst[:, tb, :], bank[pb][:, 0:256], [b_bank[pb]], [b_vst])
        sp.dma(VS[par, t0:t0 + T, :].rearrange("(b p) c -> p b c", p=128), vst, reads=[b_vst], sembuf=b_vst)

    def mixer(l, t):
        t0 = t * T
        par = l % 2
        scale = HD ** -0.5
        sp.dma(qv, QS[par, :, :, t0:t0 + T].rearrange("h d t -> d h t"), writes=[b_q], sembuf=b_q)
        lo, hi = max(t0 - 128, 0), min(t0 + T + 128, S)
        klo = lo - (t0 - 128)
        sp.dma(kv[:, :, klo:klo + (hi - lo)], KS[par, :, :, lo:hi].rearrange("g d t -> d g t"), writes=[b_k], sembuf=b_k)
        blo = klo // 128
        nb = (hi - lo) // 128
        sp.dma(vv[:, blo:blo + nb, :], VS[par, lo:hi, :].rearrange("(b p) c -> p b c", p=128), writes=[b_v], sembuf=b_v)
        sp.dma(ftv, FS[par, :, :, t0:t0 + T].rearrange("g e t -> e g t"), writes=[b_ft], sembuf=b_ft)
        KM = int(os.environ.get("KM", "9"))
        if KM < 1:
            return
        units = [(qb, g) for qb in range(4 if KM >= 2 else 0) for g in range(2)]

        def geom(qb):
            n = t * 4 + qb
            dl = [d for d in (-1, 0, 1) if 0 <= n + d < S // 128]
            return dl, len(dl), (dl[0] + 1) * 128, (qb + dl[0] + 1) * 128

        def scores_softmax(ui):
            qb, g = units[ui]
            pr = ui % 2
            dl, nkb, mo, kcol0 = geom(qb)
            nk = nkb * 128
            sm = small[:, pr * 32:(pr + 1) * 32]
            mx, negm, sums, esi, es, den, rr = (sm[:, 4 * i:4 * i + 4] for i in range(7))
            bmx, bnegm, bsums, besi, bes, bden, brr = b_sm2[pr]
            pvp = pvs[pr]
            for h in range(4):
                hd = g * 4 + h
                P.mm(bank[h][:, 0:nk], b_bank[h], qv[:, hd, qb * 128:(qb + 1) * 128], kv[:, g, kcol0:kcol0 + nk],
                     reads=[b_q, b_k], start=True, stop=False)
                P.mm(bank[h][:, 0:nk], b_bank[h], identb, maskb[:, mo:mo + nk], reads=[b_const], start=False, stop=True)
            scv = PS[:, 0:4 * 512].rearrange("p (h k) -> p h k", h=4)[:, :, 0:nk]
            dve.op(lambda e: e.tensor_reduce(out=mx, in_=scv, axis=AX.X, op=ALU.max), reads=b_bank[0:4], writes=[bmx])
            ns = nsinkb[:, l, g * 4:(g + 1) * 4]
            dve.op(lambda e: e.scalar_tensor_tensor(out=negm, in0=mx, scalar=-scale, in1=ns, op0=ALU.mult, op1=ALU.min),
                   reads=[bmx, b_const], writes=[bnegm])
            for h in range(4):
                act.op(lambda e, h=h: e.activation(out=pvp[:, h, 0:nk], in_=bank[h][:, 0:nk], func=AF.Exp,
                                                   bias=negm[:, h:h + 1], scale=scale, accum_out=sums[:, h:h + 1]),
                       reads=[b_bank[h], bnegm], writes=[b_ps[pr], bsums])
            for h in range(4):
                hd_ = g * 4 + h
                act.op(lambda e, h=h, hd_=hd_: e.activation(out=es[:, h:h + 1], in_=negm[:, h:h + 1], func=AF.Exp, bias=sinkb[:, l, hd_:hd_ + 1]),
                       reads=[bnegm, b_const], writes=[bes])

        def softmax_tail(ui):
            qb, g = units[ui]
            pr = ui % 2
            sm = small[:, pr * 32:(pr + 1) * 32]
            mx, negm, sums, esi, es, den, rr = (sm[:, 4 * i:4 * i + 4] for i in range(7))
            bmx, bnegm, bsums, besi, bes, bden, brr = b_sm2[pr]
            dve.op(lambda e: e.tensor_tensor(out=den, in0=sums, in1=es, op=ALU.add), reads=[bsums, bes], writes=[bden])
            dve.op(lambda e: e.reciprocal(out=rr, in_=den), reads=[bden], writes=[brr])
            for h in range(4):
                dg = diag2[:, pr * 4 + h, :]
                act.op(lambda e, h=h, dg=dg: e.activation(out=dg, in_=identb, func=AF.Identity, scale=rr[:, h:h + 1]),
                       reads=[brr, b_const], writes=[b_diag2[pr * 4 + h]])

        def ptpv(ui):
            qb, g = units[ui]
            pr = ui % 2
            dl, nkb, mo, kcol0 = geom(qb)
            pvp = pvs[pr]
            for kb in range(nkb):
                pb = 4 + (kb % 2)
                for h in range(4):
                    P.mm(bank[pb][:, h * 128:(h + 1) * 128], b_bank[pb], pvp[:, h, kb * 128:(kb + 1) * 128], diag2[:, pr * 4 + h, :],
                         reads=[b_ps[pr], b_diag2[pr * 4 + h]], start=True, stop=True, signal=(h == 3))
                copy_op(evac_engine(), ptv[:, kb, :], bank[pb], [b_bank[pb]], [b_pt[kb]])
            items = []
            for kb in range(nkb):
                blk = qb + dl[kb] + 1
                items.append((vv[:, blk, g * 128:(g + 1) * 128], ptv[:, kb, :]))
            P.mm_group(bank[7], b_bank[7], items, reads=[b_v] + b_pt[:nkb])
            o = aoT[:, g * 4:(g + 1) * 4, qb * 128:(qb + 1) * 128]
            i_ = bank[7].rearrange("p (h q) -> p h q", h=4)
            dve.op(lambda e: e.tensor_copy(out=o, in_=i_), reads=[b_bank[7]], writes=b_ao[g * 4:(g + 1) * 4])
            s = g % 2
            act.op(lambda e: e.activation(out=sqs[s].rearrange("p (h q) -> p h q", h=4), in_=o, func=AF.Square),
                   reads=b_ao[g * 4:(g + 1) * 4], writes=[b_sq[s]])
            for h in range(4):
                P.mm(bank[6][:, qb * 128:(qb + 1) * 128], b_bank[6], onesb, sqs[s][:, h * 128:(h + 1) * 128],
                     reads=[b_sq[s], b_const], start=(g == 0 and h == 0), stop=(g == 1 and h == 3), signal=(h == 3),
                     skip_group_check=True)

        for ui in range(len(units)):
            scores_softmax(ui)
            if ui >= 1:
                ptpv(ui - 1)
            softmax_tail(ui)
        if units:
            ptpv(len(units) - 1)
        if KM < 3:
            return
        finish_rstd(1024)
        bgr = rows["bg"] + l * 2
        for hd in range(8):
            s = hd % 4
            eng = dve if hd % 3 != 2 else pool
            eng.op(lambda e, hd=hd, s=s: e.tensor_tensor(out=tmps[s], in0=aoT[:, hd, :], in1=rstd_t, op=ALU.mult),
                   reads=[b_ao[hd], b_rstd], writes=[b_tmp[s]])
            act.op(lambda e, hd=hd, s=s: e.activation(out=hT[:, hd, :], in_=tmps[s], func=AF.Identity, scale=vecT[:, hd, bgr:bgr + 1]),
                   reads=[b_tmp[s], b_const], writes=[b_hT[hd]])
        stats_rstd([(ftv[:, g, :], [b_ft]) for g in range(8)], 1024)
        for g in range(8):
            s = g % 4
            eng = dve if g % 3 != 2 else pool
            eng.op(lambda e, g=g, s=s: e.tensor_tensor(out=tmps[s], in0=ftv[:, g, :], in1=rstd_t, op=ALU.mult),
                   reads=[b_ft, b_rstd], writes=[b_tmp[s]])
            act.op(lambda e, g=g, s=s: e.activation(out=hT[:, 8 + g, :], in_=tmps[s], func=AF.Identity, scale=vecT[:, g, bgr + 1:bgr + 2]),
                   reads=[b_tmp[s], b_const], writes=[b_hT[8 + g]])
        if KM < 4:
            return
        for i in range(KC):
            w, bw = wload(Wo[l, i], lambda a: a[:, 0:KC * 128].rearrange("p (k c) -> p k c", k=KC), ("out", l))
            ob = 4 + (i % 2)
            P.mm_group(bank[ob], b_bank[ob], [(w[:, cc, :], hT[:, cc, :]) for cc in range(KC)], reads=[bw], item_reads=b_hT)
            evac_y(i, ob)
        postnorm(l, 1)

    def load_x_dma(t):
        t0 = t * T
        sp.dma(xtok, x_d[t0:t0 + T, :].rearrange("(b p) d -> p b d", p=128), writes=[b_xtok] + b_yT, sembuf=b_xtok)

    def load_x_tokmajor(t):
        for c in range(KC):
            pb = c % 4
            for tb in range(4):
                o = bank[pb][:, tb * 128:(tb + 1) * 128]
                i_ = xtok[:, tb, c * 128:(c + 1) * 128]
                pe.op(lambda e, o=o, i_=i_: e.transpose(o, i_, ident32), reads=[b_xtok, b_const], writes=[b_bank[pb]], signal=(tb == 3))
            copy_op(evac_engine(), xT[:, c, :], bank[pb], [b_bank[pb]], [b_xT[c]])

    def load_xT(t):
        t0 = t * T
        sp.dma(xT, XS[:, :, t0:t0 + T].rearrange("c p t -> p c t"), writes=b_xT, sembuf=b_xT[0])

    def store_y(t):
        t0 = t * T
        for tb in range(4):
            for cg in range(4):
                pb = (tb * 4 + cg) % 4
                for k in range(4):
                    c = cg * 4 + k
                    o = bank[pb][:, k * 128:(k + 1) * 128]
                    i_ = xT[:, c, tb * 128:(tb + 1) * 128]
                    pe.op(lambda e, o=o, i_=i_: e.transpose(o, i_, ident32), reads=[b_xT[c], b_const], writes=[b_bank[pb]], signal=(k == 3))
                copy_op(evac_engine(), xtok[:, tb, cg * 512:(cg + 1) * 512], bank[pb], [b_bank[pb]], [b_xtok] + b_yT)
        if int(os.environ.get("KY", "1")):
            sp.dma(y_d[t0:t0 + T, :].rearrange("(b p) d -> p b d", p=128), xtok, reads=[b_xtok], sembuf=b_xtok)

    def fourier(l):
        par = l % 2
        o_ = [0]

        def fc(nbytes):
            a = arena[:, o_[0] // 2:(o_[0] + nbytes) // 2]
            o_[0] += nbytes
            return a
        uTs = [fc(S * 2) for _ in range(2)]
        FTs = [fc(S * 2) for _ in range(2)]
        data1 = fc(2 * 64 * 128 * 2)[0:NA].rearrange("p (b r e) -> p r e b", r=2, e=64)
        data2 = fc(2 * NA * 128 * 2).rearrange("p (e r a) -> p r a e", r=2, a=NA)
        tcp = fc(NA * 128 * 2).rearrange("p (a b) -> p a b", a=NA)
        tsp = fc(NA * 128 * 2).rearrange("p (a b) -> p a b", a=NA)
        wl = [fc(512).bitcast(F32) for _ in range(2)]
        Gb = [fc(512).rearrange("p (r e) -> p r e", r=2) for _ in range(2)]
        assert o_[0] <= main_bytes, (o_[0], main_bytes)
        b_u = [getbuf("fu%d" % i) for i in range(2)]
        b_FT = [getbuf("fF%d" % i) for i in range(2)]
        b_d1, b_d2, b_tt = getbuf("fd1"), getbuf("fd2"), getbuf("ftt")
        b_wl = [getbuf("fwl%d" % i) for i in range(2)]
        b_G = [getbuf("fG%d" % i) for i in range(2)]
        pool.dma(tcp, tcp_d.rearrange("p (a b) -> p a b", a=NA), writes=[b_tt], sembuf=b_tt, max_dma_last_dim=4096)
        pool.dma(tsp, tsp_d.rearrange("p (a b) -> p a b", a=NA), writes=[b_tt], sembuf=b_d1, max_dma_last_dim=4096)
        pbk = [0]

        def nb():
            pbk[0] = (pbk[0] + 1) % 6
            return pbk[0]
        for g in range(NG):
            s = g % 2
            sp.dma(wl[s], fw_d[l, g], writes=[b_wl[s]], sembuf=b_wl[s])
            sp.dma(uTs[s], US[par, g], writes=[b_u[s]], sembuf=b_u[s])
            gbk = 6 + s
            P.mm(bank[gbk][:, 0:128], b_bank[gbk], ccs, wl[s], reads=[b_wl[s], b_const], start=True, stop=True, signal=False)
            P.mm(bank[gbk][:, 128:256], b_bank[gbk], scs, wl[s], reads=[b_wl[s], b_const], start=True, stop=True)
            o = Gb[s]
            i_ = bank[gbk][:, 0:256].rearrange("p (r e) -> p r e", r=2)
            dve.op(lambda e, o=o, i_=i_: e.tensor_copy(out=o, in_=i_), reads=[b_bank[gbk]], writes=[b_G[s]])
            uT = uTs[s]
            KF = int(os.environ.get("KF", "9"))
            for eh in range(2 if KF >= 2 else 0):
                for b0 in range(0, 128, 4):
                    pb = nb()
                    for bi in range(4):
                        b = b0 + bi
                        lhs = uT[:, b::128]
                        rhs = Gb[s][:, :, eh * 64:(eh + 1) * 64]
                        P.mm(bank[pb][0:NA, bi * 128:(bi + 1) * 128], b_bank[pb], lhs, rhs, reads=[b_u[s], b_G[s]],
                             start=True, stop=True, signal=(bi == 3))
                    o = data1[:, :, :, b0:b0 + 4].rearrange("p r e b -> p b (r e)")
                    i_ = bank[pb][0:NA, :].rearrange("p (b x) -> p b x", b=4)
                    copy_op(evac_engine(), o, i_, [b_bank[pb]], [b_d1])
                for e0 in range(0, 64 if KF >= 3 else 0, 2):
                    pb = nb()
                    ne = min(512 // (2 * NA), 2) if 2 * NA * 2 <= 512 else 1
                    for ei in range(2):
                        e_ = e0 + ei
                        oo = bank[pb][:, ei * 2 * NA:(ei + 1) * 2 * NA]
                        P.mm(oo, b_bank[pb], data1[:, 0, e_, :], r1, reads=[b_d1, b_const], start=True, stop=False)
                        P.mm(oo, b_bank[pb], data1[:, 1, e_, :], r2, reads=[b_d1, b_const], start=False, stop=True, signal=(ei == 1))
                    ee = eh * 64 + e0
                    o = data2[:, :, :, ee:ee + 2].rearrange("p r a e -> p e (r a)")
                    i_ = bank[pb][:, 0:2 * 2 * NA].rearrange("p (e x) -> p e x", e=2)
                    copy_op(evac_engine(), o, i_, [b_bank[pb]], [b_d2])
            FT = FTs[s].rearrange("p (b a) -> p a b", a=NA)
            for a0 in range(0, NA if KF >= 4 else 0, 4):
                pb = nb()
                na = min(4, NA - a0)
                for ai in range(na):
                    a_ = a0 + ai
                    oo = bank[pb][:, ai * 128:(ai + 1) * 128]
                    P.mm(oo, b_bank[pb], data2[:, 0, a_, :], tcp[:, a_, :], reads=[b_d2, b_tt], start=True, stop=False)
                    P.mm(oo, b_bank[pb], data2[:, 1, a_, :], tsp[:, a_, :], reads=[b_d2, b_tt], start=False, stop=True, signal=(ai == na - 1))
                o = FT[:, a0:a0 + na, :]
                i_ = bank[pb][:, 0:na * 128].rearrange("p (a b) -> p a b", a=na)
                copy_op(evac_engine(), o, i_, [b_bank[pb]], [b_FT[s]])
            sp.dma(FS[par, g], FTs[s], reads=[b_FT[s]], sembuf=b_FT[s])

    STOP = int(os.environ.get("KSTOP", "99"))
    marks = {}
    MARKS.clear()
    MARKS.update({"_": marks})
    rows = prologue()
    marks["prologue_end"] = pe.cnt
    P.barrier()
    pool_free[0] = False
    if STOP >= 3:
        load_x_dma(0)
    for t in range(NT if STOP >= 3 else 0):
        load_x_tokmajor(t)
        if STOP >= 4:
            ffn(0, 0, 0)
        nxtA = (lambda t=t: load_x_dma(t + 1)) if t + 1 < NT else None
        if STOP >= 5:
            proj(0, t, nxtA)
        elif nxtA:
            nxtA()
        marks["A_t%d" % t] = pe.cnt
        P.barrier()
    pool_free[0] = True
    for l in range(L if STOP >= 6 else 0):
        fourier(l)
        marks["F%d" % l] = pe.cnt
        P.barrier()
        if STOP < 7:
            break
        load_xT(0)
        for t in range(NT):
            mixer(l, t)
            marks["M%d_t%d" % (l, t)] = pe.cnt
            P.barrier()
            if STOP < 8:
                continue
            KB = int(os.environ.get("KB", "9"))
            ffn(l, 1, 2)
            if l + 1 < L:
                nxtB = (lambda t=t: load_xT(t + 1)) if t + 1 < NT else None
                if KB >= 2:
                    ffn(l + 1, 0, 0)
                    proj(l + 1, t, nxtB)
                elif nxtB:
                    nxtB()
            else:
                if KB >= 3:
                    store_y(t)
                if t + 1 < NT:
                    load_xT(t + 1)
            marks["B%d_t%d" % (l, t)] = pe.cnt
            P.barrier()
    P.finalize()
    return nc


_CACHE = {}
MARKS = {}


def run_cores(cfg, per_core_inputs, n_cores):
    key = (cfg.S, cfg.DFF, cfg.L)
    if key not in _CACHE:
        _CACHE[key] = build(cfg)
    nc = _CACHE[key]
    consts = host_constants(cfg)
    in_maps = []
    for ci in range(n_cores):
        m = dict(per_core_inputs[ci])
        m.update(consts)
        in_maps.append(m)
    res = run_bass_kernel_spmd(nc, in_maps, core_ids=list(range(n_cores)))
    if int(os.environ.get("KDBG", "0")):
        return res.results
    return [r["y"] for r in res.results]


def kernel(x_prompt, x_sample, c_prompt, c_sample, w_mod, b_mod, pre_g, post_g,
           ffn_w_gate, ffn_w_up, ffn_w_down, w_in, attn_sink, fourier_w, branch_g, w_out):
    f = lambda a: np.ascontiguousarray(np.asarray(a, dtype=np.float32))
    xs = [f(x_prompt)[0]] + [f(x_sample)[i] for i in range(x_sample.shape[0])]
    cs = [f(c_prompt)[0:1]] + [f(c_sample)[i:i + 1] for i in range(c_sample.shape[0])]
    S = xs[0].shape[0]
    cfg = Cfg(S=S, DFF=ffn_w_gate.shape[-1], L=w_mod.shape[0])
    shared = {"w_mod": f(w_mod), "b_mod": f(b_mod), "pre_g": f(pre_g), "post_g": f(post_g),
              "ffn_w_gate": f(ffn_w_gate), "ffn_w_up": f(ffn_w_up), "ffn_w_down": f(ffn_w_down),
              "w_in": f(w_in), "attn_sink": f(attn_sink), "fourier_w": f(fourier_w),
              "branch_g": f(branch_g), "w_out": f(w_out)}
    n_cores = 8
    per_core = []
    for ci in range(n_cores):
        si = ci if ci < len(xs) else 0
        m = dict(shared)
        m["x"] = xs[si]
        m["c"] = cs[si]
        per_core.append(m)
    ys = run_cores(cfg, per_core, n_cores)
    y_prompt = ys[0][None].astype(np.float32)
    y_sample = np.stack(ys[1:5], axis=0).astype(np.float32)
    return (y_prompt, y_sample)
```

```python
import math
import os
import numpy as np
import concourse.bass as bass
import concourse.mybir as mybir
from concourse.bass_utils import run_bass_kernel_spmd

F32 = mybir.dt.float32
BF16 = mybir.dt.bfloat16
AF = mybir.ActivationFunctionType
ALU = mybir.AluOpType
AX = mybir.AxisListType

D = 2048
KC = 16
T = 512
HD = 128
NQH = 8
NKV = 2
NG = 8
EPS = 1e-6
NEG = -1e30
VROWS = 40


class Tok:
    __slots__ = ("sem", "val", "key")

    def __init__(self, sem, val, key):
        self.sem, self.val, self.key = sem, val, key


class Buf:
    __slots__ = ("name", "w", "rs", "dsem", "dcnt", "excl")

    def __init__(self, name, excl=False):
        self.name = name
        self.excl = excl
        self.w = {}
        self.rs = {}
        self.dsem = None
        self.dcnt = 0


class Eng:
    def __init__(self, prog, name):
        self.prog = prog
        self.name = name
        self.ops = []
        self.sem = prog.nc.alloc_semaphore(name="e_" + name)
        self.key = "e_" + name
        self.cnt = 0
        self.seen = {}
        self.pend_r, self.pend_w = [], []

    def wait(self, tok):
        if tok is None:
            return
        if self.name == "pe" and tok.key == self.key:
            return
        if self.seen.get(tok.key, 0) >= tok.val:
            return
        self.seen[tok.key] = tok.val
        sem, val = tok.sem, tok.val
        self.ops.append(lambda e: e.wait_ge(sem, val))

    def deps(self, reads, writes):
        for b in reads:
            for w_ in b.w.values():
                self.wait(w_)
            if b.excl:
                for r in b.rs.values():
                    self.wait(r)
        for b in writes:
            for w_ in b.w.values():
                self.wait(w_)
            for r in b.rs.values():
                self.wait(r)

    def record(self, tok, reads, writes):
        for b in reads:
            if b.excl:
                b.w[tok.key] = tok
                b.rs = {}
                continue
            o = b.rs.get(tok.key)
            if o is None or o.val < tok.val:
                b.rs[tok.key] = tok
        for b in writes:
            b.w[tok.key] = tok
            b.rs = {}

    def op(self, fn, reads=(), writes=(), signal=True):
        self.deps(reads, writes)
        if not signal:
            self.ops.append(fn)
            self.pend_r.extend(reads)
            self.pend_w.extend(writes)
            return None
        self.cnt += 1
        sem = self.sem
        self.ops.append(lambda e: fn(e).then_inc(sem, 1))
        tok = Tok(sem, self.cnt, self.key)
        self.record(tok, list(reads) + self.pend_r, list(writes) + self.pend_w)
        self.pend_r, self.pend_w = [], []
        self.prog.last[self.key] = tok
        return tok

    def dma(self, out_ap, in_ap, reads=(), writes=(), sembuf=None, track_last=True, **kw):
        self.deps(reads, writes)
        sb = sembuf
        if sb.dsem is None:
            sb.dsem = self.prog.nc.alloc_semaphore(name="d_" + sb.name)
        key = "d_" + sb.name
        if sb.dcnt > 0:
            self.wait(Tok(sb.dsem, sb.dcnt, key))
        sb.dcnt += 16
        sem, val = sb.dsem, sb.dcnt
        self.ops.append(lambda e: e.dma_start(out=out_ap, in_=in_ap, **kw).then_inc(sem, 16))
        tok = Tok(sem, val, key)
        self.record(tok, reads, writes)
        if track_last:
            self.prog.last[key] = tok
        return tok


class Prog:
    def __init__(self):
        self.nc = bass.Bass("TRN2", target_bir_lowering=False)
        self.last = {}
        self.pe = Eng(self, "pe")
        self.act = Eng(self, "act")
        self.dve = Eng(self, "dve")
        self.pool = Eng(self, "pool")
        self.sp = Eng(self, "sp")
        self.engs = [self.pe, self.act, self.dve, self.pool, self.sp]

    def mm(self, out_ap, out_buf, lhsT, rhs, reads, start, stop, signal=None, **kw):
        pe = self.pe
        if signal is None:
            signal = stop
        if start:
            pe.deps(reads, [out_buf])
        else:
            pe.deps(reads, [])
        if signal:
            pe.cnt += 1
            sem = pe.sem
            pe.ops.append(lambda e: e.matmul(out_ap, lhsT, rhs, start=start, stop=stop, **kw).then_inc(sem, 1))
            tok = Tok(sem, pe.cnt, pe.key)
            pe.record(tok, list(reads) + pe.pend_r, [out_buf] + pe.pend_w)
            pe.pend_r, pe.pend_w = [], []
            self.last[pe.key] = tok
            return tok
        pe.ops.append(lambda e: e.matmul(out_ap, lhsT, rhs, start=start, stop=stop, **kw))
        pe.pend_r.extend(reads)
        pe.pend_w.append(out_buf)
        return None

    def mm_group(self, out_ap, out_buf, items, reads, item_reads=None):
        pe = self.pe
        pe.deps(reads, [out_buf])
        n = len(items)
        if item_reads is not None:
            reads = list(reads) + list(item_reads)
        for i, (l, r) in enumerate(items):
            st, last = (i == 0), (i == n - 1)
            if item_reads is not None:
                pe.deps([item_reads[i]], [])
            if last:
                pe.cnt += 1
                sem = pe.sem
                pe.ops.append(lambda e, l=l, r=r, st=st: e.matmul(out_ap, l, r, start=st, stop=True).then_inc(sem, 1))
            else:
                pe.ops.append(lambda e, l=l, r=r, st=st: e.matmul(out_ap, l, r, start=st, stop=False))
        tok = Tok(pe.sem, pe.cnt, pe.key)
        pe.record(tok, list(reads) + pe.pend_r, [out_buf] + pe.pend_w)
        pe.pend_r, pe.pend_w = [], []
        self.last[pe.key] = tok
        return tok

    def barrier(self):
        toks = list(self.last.values())
        for e in self.engs:
            for t in toks:
                e.wait(t)

    def finalize(self):
        nc = self.nc
        for t in list(self.last.values()):
            self.sp.wait(t)
        with nc.allow_non_contiguous_dma(reason="small vector / layout loads"):
            with nc.Block() as block:
                def mk(engw):
                    def body(e):
                        for f in engw.ops:
                            f(e)
                    return body
                block.tensor(mk(self.pe))
                block.scalar(mk(self.act))
                block.vector(mk(self.dve))
                block.gpsimd(mk(self.pool))
                block.sync(mk(self.sp))
        return nc


class Cfg:
    def __init__(self, S=8192, DFF=5632, L=2):
        self.S, self.DFF, self.L = S, DFF, L
        self.NT = S // T
        self.NJ = DFF // 128
        self.NA = S // 128
        self.NBLK = S // 128


def host_constants(cfg):
    S, NA = cfg.S, cfg.NA
    inv_freq = (10000.0 ** (-np.arange(0, HD, 2, dtype=np.float32) / HD)).astype(np.float32)
    ang = np.arange(S, dtype=np.float32)[:, None] * inv_freq[None, :]
    cos, sin = np.cos(ang).astype(np.float32), np.sin(ang).astype(np.float32)
    ropeC = np.concatenate([cos, cos], axis=1).T.copy()
    ropeS = np.concatenate([sin, sin], axis=1).T.copy()
    pm = np.zeros((128, 128), np.float32)
    for m in range(64):
        pm[m + 64, m] = -1.0
    for m in range(64, 128):
        pm[m - 64, m] = 1.0
    i = np.arange(128)[:, None]
    j = np.arange(384)[None, :]
    mask = np.where((j >= i) & (j <= i + 256), 0.0, NEG).astype(np.float32)
    c = np.arange(128)
    scale = 1.0 / math.sqrt(S * 128.0)
    angc = 2.0 * np.pi * ((c[:, None] * c[None, :]) % 128) / 128.0
    ccs = (np.cos(angc) * scale).astype(np.float32)
    scs = (-np.sin(angc) * scale).astype(np.float32)
    a = np.arange(NA)
    anga = 2.0 * np.pi * ((a[:, None] * a[None, :]) % NA) / NA
    ca, sa = np.cos(anga), np.sin(anga)
    r1 = np.concatenate([ca, -sa], axis=1).astype(np.float32)
    r2 = np.concatenate([sa, ca], axis=1).astype(np.float32)
    b = np.arange(128)
    sp = np.arange(S)
    angt = 2.0 * np.pi * ((b[:, None].astype(np.int64) * sp[None, :].astype(np.int64)) % S) / S
    tc = np.cos(angt).reshape(128, 128, NA).transpose(0, 2, 1)
    ts = np.sin(angt).reshape(128, 128, NA).transpose(0, 2, 1)
    return {
        "ropeC": ropeC, "ropeS": ropeS, "pmat": pm, "ident": np.eye(128, dtype=np.float32),
        "ones": np.ones((128, 128), np.float32), "maskb": mask, "ccs": ccs, "scs": scs,
        "r1": r1, "r2": r2,
        "tcp": np.ascontiguousarray(tc).astype(np.float32).reshape(128, NA * 128),
        "tsp": np.ascontiguousarray(ts).astype(np.float32).reshape(128, NA * 128),
    }


def build(cfg):
    S, DFF, L, NT, NJ, NA = cfg.S, cfg.DFF, cfg.L, cfg.NT, cfg.NJ, cfg.NA
    P = Prog()
    nc = P.nc
    pe, act, dve, pool, sp = P.pe, P.act, P.dve, P.pool, P.sp

    def din(name, shape):
        return nc.dram_tensor(name, list(shape), F32, kind="ExternalInput").ap()

    x_d = din("x", [S, D])
    c_d = din("c", [1, D])
    wmod_d = din("w_mod", [L, D, 9 * D])
    bmod_d = din("b_mod", [L, 9 * D])
    preg_d = din("pre_g", [L, 3, D])
    postg_d = din("post_g", [L, 3, D])
    wg_d = din("ffn_w_gate", [L, 2, D, DFF])
    wu_d = din("ffn_w_up", [L, 2, D, DFF])
    wd_d = din("ffn_w_down", [L, 2, DFF, D])
    win_d = din("w_in", [L, D, 2560])
    sink_d = din("attn_sink", [L, 8])
    fw_d = din("fourier_w", [L, 8, 128, 128])
    bg_d = din("branch_g", [L, 2, 1024])
    wo_d = din("w_out", [L, D, D])
    ropeC_d = din("ropeC", [128, S])
    ropeS_d = din("ropeS", [128, S])
    pmat_d = din("pmat", [128, 128])
    ident_d = din("ident", [128, 128])
    ones_d = din("ones", [128, 128])
    maskb_d = din("maskb", [128, 384])
    ccs_d = din("ccs", [128, 128])
    scs_d = din("scs", [128, 128])
    r1_d = din("r1", [NA, 2 * NA])
    r2_d = din("r2", [NA, 2 * NA])
    tcp_d = din("tcp", [128, NA * 128])
    tsp_d = din("tsp", [128, NA * 128])
    y_d = nc.dram_tensor("y", [S, D], F32, kind="ExternalOutput").ap()

    DBG = bool(int(os.environ.get("KDBG", "0")))

    def dscr(name, shape, dt=BF16):
        return nc.dram_tensor(name, list(shape), dt, kind=("ExternalOutput" if DBG and name in ("QS", "KS", "VS", "US", "FS", "XS") else "Internal")).ap()

    Wgu = dscr("Wgu", [L, 2, NJ, 128, 2, KC, 128])
    Wd = dscr("Wd", [L, 2, 16, 128, NJ, 128])
    Win = dscr("Win", [L, 18, 128, KC, 128])
    Wv = dscr("Wv", [L, 128, KC, 256])
    Wo = dscr("Wo", [L, 16, 128, KC, 128])
    QS = dscr("QS", [2, 8, 128, S])
    KS = dscr("KS", [2, 2, 128, S])
    VS = dscr("VS", [2, S, 256])
    US = dscr("US", [2, 8, 128, S])
    FS = dscr("FS", [2, 8, 128, S])
    XS = dscr("XS", [KC, 128, S], F32)

    def sb(name, shape, dt):
        return nc.alloc_sbuf_tensor("s_" + name, list(shape), dt).ap()

    ident32 = sb("ident32", [128, 128], F32)
    pmat = sb("pmat", [128, 128], F32)
    ccs = sb("ccs", [128, 128], F32)
    scs = sb("scs", [128, 128], F32)
    identb = sb("identb", [128, 128], BF16)
    onesb = sb("onesb", [128, 128], BF16)
    maskb = sb("maskb", [128, 384], BF16)
    r1 = sb("r1", [NA, 2 * NA], BF16)
    r2 = sb("r2", [NA, 2 * NA], BF16)
    epst = sb("epst", [128, 1], F32)
    vecT = sb("vecT", [128, KC, VROWS], F32)
    cact = sb("cact", [128, KC, 2], F32)
    modT = sb("modT", [128, L, 9, KC], F32)
    gsT = sb("gsT", [128, L, 3, KC], F32)
    coefT = sb("coefT", [128, L, 3, KC], F32)
    sinkb = sb("sinkb", [128, L, 8], F32)
    nsinkb = sb("nsinkb", [128, L, 8], F32)
    small = sb("small", [128, 64], F32)
    diag2 = sb("diag2", [128, 8, 128], BF16)
    b_const = Buf("const")
    b_small = [Buf("sm%d" % i) for i in range(8)]
    b_diag2 = [Buf("diag%d" % i) for i in range(8)]
    b_sm2 = [[Buf("smx%d_%d" % (p_, i)) for i in range(7)] for p_ in range(2)]
    b_ps = [Buf("pA"), Buf("pB")]

    XT_B, HT_B, ACT_B, YT_B = 32768, 16384, NJ * 1024 if NJ * 1024 > 45056 else 45056, 32768
    WSLOT = max(2 * KC * 128 * 2, NJ * 128 * 2, KC * 256 * 2)
    NW = 3
    SCR_B = 4 * 1024 + 4 * 2048 + 2048 + 4096
    tot = XT_B + HT_B + ACT_B + YT_B + NW * WSLOT + SCR_B
    arena = nc.alloc_sbuf_tensor("arena", [128, tot // 2], BF16).ap()
    off = [0]

    def carve(nbytes):
        a = arena[:, off[0] // 2:(off[0] + nbytes) // 2]
        off[0] += nbytes
        return a

    xT = carve(XT_B).bitcast(F32).rearrange("p (c t) -> p c t", c=KC)
    hT = carve(HT_B).rearrange("p (c t) -> p c t", c=KC)
    act_raw = carve(ACT_B)
    yT_raw = carve(YT_B)
    yT = yT_raw.bitcast(F32).rearrange("p (c t) -> p c t", c=KC)
    wslots = [carve(WSLOT) for _ in range(NW)]
    sqs = [carve(1024) for _ in range(4)]
    tmps = [carve(2048).bitcast(F32) for _ in range(4)]
    rstd_t = carve(2048).bitcast(F32)
    sd_t = rstd_t
    ropeCt = carve(2048).bitcast(F32)
    ropeSt = carve(2048).bitcast(F32)
    main_bytes = off[0]

    actc = act_raw[:, 0:NJ * 512].rearrange("p (j t) -> p j t", j=NJ)
    qv = act_raw[:, 0:8 * 512].rearrange("p (h t) -> p h t", h=8)
    kv = act_raw[:, 8 * 512:8 * 512 + 2 * 768].rearrange("p (g t) -> p g t", g=2)
    vv = act_raw[:, 11 * 512:11 * 512 + 6 * 256].rearrange("p (b c) -> p b c", b=6)
    pv = act_raw[:, 14 * 512:14 * 512 + 4 * 384].rearrange("p (h k) -> p h k", h=4)
    ptv = act_raw[:, 17 * 512:17 * 512 + 3 * 512].rearrange("p (k t) -> p k t", k=3)
    pvs = [pv, yT_raw[:, 0:4 * 384].rearrange("p (h k) -> p h k", h=4)]
    aoT = act_raw[:, 20 * 512:36 * 512].bitcast(F32).rearrange("p (h t) -> p h t", h=8)
    ftv = act_raw[:, 36 * 512:44 * 512].rearrange("p (g t) -> p g t", g=8)
    qst = act_raw[:, 0:10 * 512].rearrange("p (h t) -> p h t", h=10)
    ust = act_raw[:, 10 * 512:18 * 512].rearrange("p (g t) -> p g t", g=8)
    vst = act_raw[:, 18 * 512:18 * 512 + 4 * 256].rearrange("p (b c) -> p b c", b=4)
    qf_t = act_raw[:, 20 * 512:20 * 512 + 2 * 1024].bitcast(F32).rearrange("p (k t) -> p k t", k=2)
    xtok = yT_raw.bitcast(F32).rearrange("p (b d) -> p b d", b=4)

    PS = nc.alloc_psum_tensor("PS", [128, 8 * 512], F32).ap()
    bank = [PS[:, i * 512:(i + 1) * 512] for i in range(8)]
    b_bank = [Buf("bank%d" % i, excl=True) for i in range(8)]

    b_xT = [Buf("xT%d" % i) for i in range(KC)]
    b_hT = [Buf("hT%d" % i) for i in range(KC)]
    b_yT = [Buf("yT%d" % i) for i in range(KC)]
    b_act = [Buf("act%d" % i) for i in range(max(NJ, 44))]
    b_w = [Buf("w%d" % i) for i in range(NW)]
    b_sq = [Buf("sq%d" % i) for i in range(4)]
    b_tmp = [Buf("tmp%d" % i) for i in range(4)]
    b_rstd = Buf("rstd")
    b_sd = Buf("sd")
    b_rope = Buf("rope")
    b_q, b_k, b_v, b_ft = Buf("q"), Buf("k"), Buf("v"), Buf("ft")
    b_p = Buf("p")
    b_pt = [Buf("pt%d" % i) for i in range(3)]
    b_ao = [Buf("ao%d" % i) for i in range(8)]
    b_qst = [Buf("qst%d" % i) for i in range(10)]
    b_ust = [Buf("ust%d" % i) for i in range(8)]
    b_vst = Buf("vst")
    b_qf = [Buf("qf%d" % i) for i in range(2)]
    b_xtok = Buf("xtok")
    b_dram = Buf("dram")

    cnt = {"w": 0, "ev": 0}
    bufcache = {}

    def getbuf(name):
        if name not in bufcache:
            bufcache[name] = Buf(name)
        return bufcache[name]

    conv_toks = {}
    pool_free = [False]

    def wload(src_ap, view_fn, grp):
        s = cnt["w"] % NW
        cnt["w"] += 1
        v = view_fn(wslots[s])
        for t_ in conv_toks.get(grp, ()):
            sp.wait(t_)
        sp.dma(v, src_ap, writes=[b_w[s]], sembuf=b_w[s])
        return v, b_w[s]

    def evac_engine():
        cnt["ev"] += 1
        return act if cnt["ev"] % 2 == 0 else dve

    def copy_op(eng, out, in_, reads, writes):
        if eng is act:
            return act.op(lambda e: e.activation(out=out, in_=in_, func=AF.Copy), reads, writes)
        return eng.op(lambda e: e.tensor_copy(out=out, in_=in_), reads, writes)

    def prologue():
        k = 0
        for dst, src in ((ident32, ident_d), (pmat, pmat_d), (ccs, ccs_d), (scs, scs_d)):
            sp.dma(dst, src, writes=[b_const], sembuf=b_small[k % 8]); k += 1
        for dst, src in ((identb, ident_d), (onesb, ones_d), (maskb, maskb_d), (r1, r1_d), (r2, r2_d)):
            pool.dma(dst, src, writes=[b_const], sembuf=b_small[k % 8]); k += 1
        sp.dma(sinkb.rearrange("p l h -> p (l h)"),
               sink_d.rearrange("(o l) h -> o (l h)", o=1).broadcast_to([128, L * 8]),
               writes=[b_const], sembuf=b_small[k % 8]); k += 1
        V = yT_raw.bitcast(F32)[0:VROWS, 0:D]
        pool.op(lambda e: e.memset(yT_raw.bitcast(F32)[0:64, 0:D], 0.0), writes=[b_xtok])
        pool.op(lambda e: e.memset(epst, EPS), writes=[b_const])
        if STOP >= 2:
            convert_weights()
        r = 0
        rows = {}
        for name, src, n in (("pre", preg_d.rearrange("l j d -> (l j) d"), 3 * L),
                             ("post", postg_d.rearrange("l j d -> (l j) d"), 3 * L),
                             ("bmod", bmod_d.rearrange("l (m d) -> (l m) d", d=D), 9 * L),
                             ("c", c_d, 1)):
            rows[name] = r
            sp.dma(V[r:r + n, :], src, writes=[b_xtok], sembuf=b_small[k % 8]); k += 1
            r += n
        rows["bg"] = r
        sp.dma(V[r:r + 2 * L, 0:1024], bg_d.rearrange("l b d -> (l b) d"), writes=[b_xtok], sembuf=b_small[k % 8]); k += 1
        r += 2 * L
        assert r <= VROWS
        for half in range(2):
            for cc in range(8):
                c = half * 8 + cc
                o = bank[half][:, cc * VROWS:(cc + 1) * VROWS]
                i_ = V[:, c * 128:(c + 1) * 128]
                pe.op(lambda e, o=o, i_=i_: e.transpose(o, i_, ident32[0:VROWS, 0:VROWS]),
                      reads=[b_xtok, b_const], writes=[b_bank[half]])
            o = vecT[:, half * 8:(half + 1) * 8, :]
            i_ = bank[half][:, 0:8 * VROWS].rearrange("p (c r) -> p c r", c=8)
            dve.op(lambda e, o=o, i_=i_: e.tensor_copy(out=o, in_=i_), reads=[b_bank[half]], writes=[b_const])
        for dup in range(2):
            o = cact[:, :, dup]
            i_ = vecT[:, :, rows["c"]]
            act.op(lambda e, o=o, i_=i_: e.activation(out=o, in_=i_, func=AF.Silu), reads=[b_const], writes=[b_const])
        dve.op(lambda e: e.tensor_scalar(out=nsinkb, in0=sinkb, scalar1=-1.0, scalar2=None, op0=ALU.mult),
               reads=[b_const], writes=[b_const])
        OCB = 4
        stg = [arena[:, 0:KC * OCB * 128 * 2].bitcast(F32).rearrange("p (k c) -> p k c", k=KC),
               act_raw[:, 0:KC * OCB * 128 * 2].bitcast(F32).rearrange("p (k c) -> p k c", k=KC)]
        b_stg = [b_act[0], b_act[1]]
        nblk = 144 // OCB
        for l in range(L):
            mb = bank[2 + (l % 2)]
            for blk in range(nblk):
                s = (l * nblk + blk) % 2
                src = wmod_d[l, :, blk * OCB * 128:(blk + 1) * OCB * 128].rearrange("(k p) c -> p k c", p=128)
                sp.dma(stg[s], src, writes=[b_stg[s]], sembuf=b_stg[s])
                for oc in range(OCB):
                    col = (blk * OCB + oc) * 2
                    for kc in range(KC):
                        P.mm(mb[:, col:col + 2], b_bank[2 + (l % 2)], stg[s][:, kc, oc * 128:(oc + 1) * 128], cact[:, kc, :],
                             reads=[b_stg[s], b_const], start=(kc == 0), stop=(kc == KC - 1),
                             signal=(kc == KC - 1 and oc == OCB - 1))
            i0 = mb[:, 0:288].rearrange("p (jm c two) -> p jm c two", jm=9, c=KC)[:, :, :, 0]
            i1 = vecT[:, :, rows["bmod"] + l * 9:rows["bmod"] + (l + 1) * 9].rearrange("p c jm -> p jm c")
            o = modT[:, l]
            dve.op(lambda e, o=o, i0=i0, i1=i1: e.tensor_tensor(out=o, in0=i0, in1=i1, op=ALU.add),
                   reads=[b_bank[2 + (l % 2)], b_const], writes=[b_const])
            for j in range(3):
                wgt = (0.5, 1.0, 0.5)[j]
                o = gsT[:, l, j, :]
                sc = modT[:, l, 3 * j + 1, :]
                pg = vecT[:, :, rows["pre"] + l * 3 + j]
                dve.op(lambda e, o=o, sc=sc, pg=pg: e.scalar_tensor_tensor(out=o, in0=sc, scalar=1.0, in1=pg, op0=ALU.add, op1=ALU.mult),
                       reads=[b_const], writes=[b_const])
                o2 = coefT[:, l, j, :]
                gt = modT[:, l, 3 * j + 2, :]
                qg = vecT[:, :, rows["post"] + l * 3 + j]
                dve.op(lambda e, o2=o2, gt=gt, qg=qg: e.scalar_tensor_tensor(out=o2, in0=gt, scalar=1.0, in1=qg, op0=ALU.add, op1=ALU.mult),
                       reads=[b_const], writes=[b_const])
                if wgt != 1.0:
                    dve.op(lambda e, o2=o2, wgt=wgt: e.tensor_scalar(out=o2, in0=o2, scalar1=wgt, scalar2=None, op0=ALU.mult),
                           reads=[b_const], writes=[b_const])
        return rows

    def convert_weights():
        cvb = [Buf("cv%d" % i) for i in range(6)]
        k = [0]
        cur = [None]

        def cv(out_ap, in_ap):
            t_ = pool.dma(out_ap, in_ap, sembuf=cvb[k[0] % 6], track_last=False, max_dma_last_dim=4096)
            conv_toks.setdefault(cur[0], {})[t_.key] = t_
            k[0] += 1

        def conv_ffn(l, f):
            cur[0] = ("ffn", l, f)
            for m, src in ((0, wg_d), (1, wu_d)):
                v = src[l, f].rearrange("(kc p) (j c) -> kc j p c", p=128, c=128)
                for kc in range(KC):
                    cv(Wgu[l, f, :, :, m, kc, :], v[kc])
            v = wd_d[l, f].rearrange("(jc p) (i c) -> jc i p c", p=128, c=128)
            for jc in range(NJ):
                cv(Wd[l, f, :, :, jc, :], v[jc])

        def conv_in(l):
            cur[0] = ("in", l)
            v = win_d[l].rearrange("(kc p) (j c) -> kc j p c", p=128, c=128)
            for kc in range(KC):
                cv(Win[l, 0:10, :, kc, :], v[kc, 0:10])
                cv(Win[l, 10:18, :, kc, :], v[kc, 12:20])
            v2 = win_d[l].rearrange("(kc p) c -> kc p c", p=128)
            for kc in range(KC):
                cv(Wv[l, :, kc, :], v2[kc, :, 1280:1536])

        def conv_out(l):
            cur[0] = ("out", l)
            v = wo_d[l].rearrange("(kc p) (j c) -> kc j p c", p=128, c=128)
            for kc in range(KC):
                cv(Wo[l, :, :, kc, :], v[kc])

        conv_ffn(0, 0)
        conv_in(0)
        for l in range(L):
            conv_out(l)
            conv_ffn(l, 1)
            if l + 1 < L:
                conv_ffn(l + 1, 0)
                conv_in(l + 1)
        for g_ in list(conv_toks):
            conv_toks[g_] = list(conv_toks[g_].values())

    def stats_rstd(srcs, dim):
        n = len(srcs)
        for i, (ap, bufs) in enumerate(srcs):
            s = i % 4
            if i % 3 == 2 and pool_free[0]:
                pool.op(lambda e, ap=ap, s=s: e.tensor_tensor(out=sqs[s], in0=ap, in1=ap, op=ALU.mult), reads=bufs, writes=[b_sq[s]])
            else:
                act.op(lambda e, ap=ap, s=s: e.activation(out=sqs[s], in_=ap, func=AF.Square), reads=bufs, writes=[b_sq[s]])
            P.mm(bank[6], b_bank[6], onesb, sqs[s], reads=[b_sq[s], b_const], start=(i == 0), stop=(i == n - 1), signal=True)
        finish_rstd(dim)

    def finish_rstd(dim):
        if int(os.environ.get("KRSQ", "0")):
            act.op(lambda e: e.activation(out=rstd_t, in_=bank[6], func=AF.Abs_reciprocal_sqrt, bias=epst, scale=1.0 / dim),
                   reads=[b_bank[6], b_const], writes=[b_rstd])
        else:
            act.op(lambda e: e.activation(out=rstd_t, in_=bank[6], func=AF.Sqrt, bias=epst, scale=1.0 / dim),
                   reads=[b_bank[6], b_const], writes=[b_rstd])
            dve.op(lambda e: e.reciprocal(out=rstd_t, in_=rstd_t), reads=[], writes=[b_rstd])

    def prenorm(l, j):
        stats_rstd([(xT[:, c, :], [b_xT[c]]) for c in range(KC)], D)
        for c in range(KC):
            s = c % 4
            eng = dve if (c % 3 != 2 or not pool_free[0]) else pool
            eng.op(lambda e, c=c, s=s: e.tensor_tensor(out=tmps[s], in0=xT[:, c, :], in1=rstd_t, op=ALU.mult),
                   reads=[b_xT[c], b_rstd], writes=[b_tmp[s]])
            act.op(lambda e, c=c, s=s: e.activation(out=hT[:, c, :], in_=tmps[s], func=AF.Identity,
                                                    scale=gsT[:, l, j, c:c + 1], bias=modT[:, l, 3 * j, c:c + 1]),
                   reads=[b_tmp[s], b_const], writes=[b_hT[c]])

    def postnorm(l, j):
        flush_stats()
        finish_rstd(D)
        for c in range(KC):
            s = c % 4
            e1 = pool if (pool_free[0] and c % 3 == 2) else dve
            e2 = pool if (pool_free[0] and c % 3 == 0) else dve
            e1.op(lambda e, c=c, s=s: e.tensor_tensor(out=tmps[s], in0=yT[:, c, :], in1=rstd_t, op=ALU.mult),
                  reads=[b_yT[c], b_rstd], writes=[b_tmp[s]])
            e2.op(lambda e, c=c, s=s: e.tensor_tensor(out=xT[:, c, :], in0=xT[:, c, :], in1=tmps[s], op=ALU.add),
                  reads=[b_tmp[s]], writes=[b_xT[c]])

    pend_stats = []

    def flush_stats():
        while pend_stats:
            s_, i_ = pend_stats.pop(0)
            P.mm(bank[6], b_bank[6], onesb, sqs[s_], reads=[b_sq[s_], b_const], start=(i_ == 0), stop=(i_ == KC - 1), signal=True)

    def evac_y(i, ob, l, j):
        s = i % 4
        flush_stats()
        act.op(lambda e, s=s, ob=ob: e.activation(out=sqs[s], in_=bank[ob], func=AF.Square), reads=[b_bank[ob]], writes=[b_sq[s]])
        act.op(lambda e, i=i, ob=ob: e.activation(out=yT[:, i, :], in_=bank[ob], func=AF.Identity, scale=coefT[:, l, j, i:i + 1]),
               reads=[b_bank[ob], b_const], writes=[b_yT[i]])
        pend_stats.append((s, i))

    def ffn(l, f, j):
        FS_ = int(os.environ.get("KFFN", "9"))
        prenorm(l, j)
        if FS_ < 2:
            return
        for jj in range(NJ):
            w, bw = wload(Wgu[l, f, jj], lambda a: a[:, 0:2 * KC * 128].rearrange("p (m k c) -> p m k c", m=2, k=KC), ("ffn", l, f))
            st = jj % 2
            gb, ub = 2 * st, 2 * st + 1
            P.mm_group(bank[gb], b_bank[gb], [(w[:, 0, kc, :], hT[:, kc, :]) for kc in range(KC)], reads=[bw], item_reads=b_hT)
            P.mm_group(bank[ub], b_bank[ub], [(w[:, 1, kc, :], hT[:, kc, :]) for kc in range(KC)], reads=[bw], item_reads=b_hT)
            act.op(lambda e, st=st, gb=gb: e.activation(out=tmps[st], in_=bank[gb], func=AF.Silu), reads=[b_bank[gb]], writes=[b_tmp[st]])
            dve.op(lambda e, st=st, ub=ub, jj=jj: e.tensor_tensor(out=actc[:, jj, :], in0=tmps[st], in1=bank[ub], op=ALU.mult),
                   reads=[b_tmp[st], b_bank[ub]], writes=[b_act[jj]])
        if FS_ < 3:
            return
        for i in range(KC):
            w, bw = wload(Wd[l, f, i], lambda a: a[:, 0:NJ * 128].rearrange("p (j c) -> p j c", j=NJ), ("ffn", l, f))
            ob = 4 + (i % 2)
            P.mm_group(bank[ob], b_bank[ob], [(w[:, jc, :], actc[:, jc, :]) for jc in range(NJ)], reads=[bw], item_reads=b_act[:NJ])
            evac_y(i, ob, l, j)
        if FS_ < 4:
            return
        postnorm(l, j)

    def proj(l, t, nxt=None):
        t0 = t * T
        par = l % 2
        prenorm(l, 1)
        sp.dma(XS[:, :, t0:t0 + T].rearrange("c p t -> p c t"), xT, reads=b_xT, sembuf=b_xT[0])
        if nxt is not None:
            nxt()
        sp.dma(ropeCt, ropeC_d[:, t0:t0 + T], writes=[b_rope], sembuf=b_rope)
        sp.dma(ropeSt, ropeS_d[:, t0:t0 + T], writes=[b_rope], sembuf=b_small[0])
        for jq in range(18):
            w, bw = wload(Win[l, jq], lambda a: a[:, 0:KC * 128].rearrange("p (k c) -> p k c", k=KC), ("in", l))
            pb = jq % 4
            P.mm_group(bank[pb], b_bank[pb], [(w[:, kc, :], hT[:, kc, :]) for kc in range(KC)], reads=[bw], item_reads=b_hT)
            if jq < 10:
                s = jq % 2
                act.op(lambda e, s=s, pb=pb: e.activation(out=qf_t[:, s, :], in_=bank[pb], func=AF.Copy), reads=[b_bank[pb]], writes=[b_qf[s]])
                P.mm(bank[7], b_bank[7], pmat, qf_t[:, s, :], reads=[b_qf[s], b_const], start=True, stop=True)
                dve.op(lambda e, s=s: e.tensor_tensor(out=tmps[s], in0=bank[7], in1=ropeSt, op=ALU.mult),
                       reads=[b_bank[7], b_rope], writes=[b_tmp[s]])
                pe_ = pool if pool_free[0] else dve
                pe_.op(lambda e, s=s: e.tensor_tensor(out=qf_t[:, s, :], in0=qf_t[:, s, :], in1=ropeCt, op=ALU.mult),
                       reads=[b_rope], writes=[b_qf[s]])
                pe_.op(lambda e, s=s, jq=jq: e.tensor_tensor(out=qst[:, jq, :], in0=qf_t[:, s, :], in1=tmps[s], op=ALU.add),
                       reads=[b_qf[s], b_tmp[s]], writes=[b_qst[jq]])
            else:
                g = jq - 10
                copy_op(evac_engine(), ust[:, g, :], bank[pb], [b_bank[pb]], [b_ust[g]])
        sp.dma(QS[par, :, :, t0:t0 + T].rearrange("h d t -> d h t"), qst[:, 0:8, :], reads=b_qst[0:8], sembuf=b_qst[0])
        sp.dma(KS[par, :, :, t0:t0 + T].rearrange("h d t -> d h t"), qst[:, 8:10, :], reads=b_qst[8:10], sembuf=b_qst[8])
        sp.dma(US[par, :, :, t0:t0 + T].rearrange("g c t -> c g t"), ust, reads=b_ust, sembuf=b_ust[0])
        w, bw = wload(Wv[l], lambda a: a[:, 0:KC * 256].rearrange("p (k c) -> p k c", k=KC), ("in", l))
        for tb in range(4):
            pb = tb % 4
            P.mm_group(bank[pb][:, 0:256], b_bank[pb], [(hT[:, kc, tb * 128:(tb + 1) * 128], w[:, kc, :]) for kc in range(KC)],
                       reads=[bw] + b_hT)
            copy_op(evac_engine(), vst[:, tb, :], bank[pb][:, 0:256], [b_bank[pb]], [b_vst])
        sp.dma(VS[par, t0:t0 + T, :].rearrange("(b p) c -> p b c", p=128), vst, reads=[b_vst], sembuf=b_vst)

    def mixer(l, t):
        t0 = t * T
        par = l % 2
        scale = HD ** -0.5
        sp.dma(qv, QS[par, :, :, t0:t0 + T].rearrange("h d t -> d h t"), writes=[b_q], sembuf=b_q)
        lo, hi = max(t0 - 128, 0), min(t0 + T + 128, S)
        klo = lo - (t0 - 128)
        sp.dma(kv[:, :, klo:klo + (hi - lo)], KS[par, :, :, lo:hi].rearrange("g d t -> d g t"), writes=[b_k], sembuf=b_k)
        blo = klo // 128
        nb = (hi - lo) // 128
        sp.dma(vv[:, blo:blo + nb, :], VS[par, lo:hi, :].rearrange("(b p) c -> p b c", p=128), writes=[b_v], sembuf=b_v)
        sp.dma(ftv, FS[par, :, :, t0:t0 + T].rearrange("g e t -> e g t"), writes=[b_ft], sembuf=b_ft)
        KM = int(os.environ.get("KM", "9"))
        if KM < 1:
            return
        units = [(qb, g) for qb in range(4 if KM >= 2 else 0) for g in range(2)]

        def geom(qb):
            n = t * 4 + qb
            dl = [d for d in (-1, 0, 1) if 0 <= n + d < S // 128]
            return dl, len(dl), (dl[0] + 1) * 128, (qb + dl[0] + 1) * 128

        def scores_softmax(ui):
            qb, g = units[ui]
            pr = ui % 2
            dl, nkb, mo, kcol0 = geom(qb)
            nk = nkb * 128
            sm = small[:, pr * 32:(pr + 1) * 32]
            mx, negm, sums, esi, es, den, rr = (sm[:, 4 * i:4 * i + 4] for i in range(7))
            bmx, bnegm, bsums, besi, bes, bden, brr = b_sm2[pr]
            pvp = pvs[pr]
            for h in range(4):
                hd = g * 4 + h
                P.mm(bank[h][:, 0:nk], b_bank[h], qv[:, hd, qb * 128:(qb + 1) * 128], kv[:, g, kcol0:kcol0 + nk],
                     reads=[b_q, b_k], start=True, stop=False)
                P.mm(bank[h][:, 0:nk], b_bank[h], identb, maskb[:, mo:mo + nk], reads=[b_const], start=False, stop=True)
            scv = PS[:, 0:4 * 512].rearrange("p (h k) -> p h k", h=4)[:, :, 0:nk]
            dve.op(lambda e: e.tensor_reduce(out=mx, in_=scv, axis=AX.X, op=ALU.max), reads=b_bank[0:4], writes=[bmx])
            ns = nsinkb[:, l, g * 4:(g + 1) * 4]
            dve.op(lambda e: e.scalar_tensor_tensor(out=negm, in0=mx, scalar=-scale, in1=ns, op0=ALU.mult, op1=ALU.min),
                   reads=[bmx, b_const], writes=[bnegm])
            for h in range(4):
                act.op(lambda e, h=h: e.activation(out=pvp[:, h, 0:nk], in_=bank[h][:, 0:nk], func=AF.Exp,
                                                   bias=negm[:, h:h + 1], scale=scale, accum_out=sums[:, h:h + 1]),
                       reads=[b_bank[h], bnegm], writes=[b_ps[pr], bsums])
            for h in range(4):
                hd_ = g * 4 + h
                act.op(lambda e, h=h, hd_=hd_: e.activation(out=es[:, h:h + 1], in_=negm[:, h:h + 1], func=AF.Exp, bias=sinkb[:, l, hd_:hd_ + 1]),
                       reads=[bnegm, b_const], writes=[bes])

        def softmax_tail(ui):
            qb, g = units[ui]
            pr = ui % 2
            sm = small[:, pr * 32:(pr + 1) * 32]
            mx, negm, sums, esi, es, den, rr = (sm[:, 4 * i:4 * i + 4] for i in range(7))
            bmx, bnegm, bsums, besi, bes, bden, brr = b_sm2[pr]
            dve.op(lambda e: e.tensor_tensor(out=den, in0=sums, in1=es, op=ALU.add), reads=[bsums, bes], writes=[bden])
            dve.op(lambda e: e.reciprocal(out=rr, in_=den), reads=[bden], writes=[brr])
            for h in range(4):
                dg = diag2[:, pr * 4 + h, :]
                act.op(lambda e, h=h, dg=dg: e.activation(out=dg, in_=identb, func=AF.Identity, scale=rr[:, h:h + 1]),
                       reads=[brr, b_const], writes=[b_diag2[pr * 4 + h]])

        def ptpv(ui):
            qb, g = units[ui]
            pr = ui % 2
            dl, nkb, mo, kcol0 = geom(qb)
            pvp = pvs[pr]
            for kb in range(nkb):
                pb = 4 + (kb % 2)
                for h in range(4):
                    P.mm(bank[pb][:, h * 128:(h + 1) * 128], b_bank[pb], pvp[:, h, kb * 128:(kb + 1) * 128], diag2[:, pr * 4 + h, :],
                         reads=[b_ps[pr], b_diag2[pr * 4 + h]], start=True, stop=True, signal=(h == 3))
                copy_op(dve, ptv[:, kb, :], bank[pb], [b_bank[pb]], [b_pt[kb]])
            items = []
            for kb in range(nkb):
                blk = qb + dl[kb] + 1
                items.append((vv[:, blk, g * 128:(g + 1) * 128], ptv[:, kb, :]))
            P.mm_group(bank[7], b_bank[7], items, reads=[b_v] + b_pt[:nkb])
            o = aoT[:, g * 4:(g + 1) * 4, qb * 128:(qb + 1) * 128]
            i_ = bank[7].rearrange("p (h q) -> p h q", h=4)
            dve.op(lambda e: e.tensor_copy(out=o, in_=i_), reads=[b_bank[7]], writes=b_ao[g * 4:(g + 1) * 4])
            s = g % 2
            pool.op(lambda e: e.tensor_tensor(out=sqs[s].rearrange("p (h q) -> p h q", h=4), in0=o, in1=o, op=ALU.mult),
                    reads=b_ao[g * 4:(g + 1) * 4], writes=[b_sq[s]])
            for h in range(4):
                P.mm(bank[6][:, qb * 128:(qb + 1) * 128], b_bank[6], onesb, sqs[s][:, h * 128:(h + 1) * 128],
                     reads=[b_sq[s], b_const], start=(g == 0 and h == 0), stop=(g == 1 and h == 3), signal=(h == 3),
                     skip_group_check=True)

        for ui in range(len(units)):
            scores_softmax(ui)
            if ui >= 1:
                ptpv(ui - 1)
            softmax_tail(ui)
        if units:
            ptpv(len(units) - 1)
        if KM < 3:
            return
        finish_rstd(1024)
        bgr = rows["bg"] + l * 2
        for hd in range(8):
            s = hd % 4
            eng = dve if hd % 3 != 2 else pool
            eng.op(lambda e, hd=hd, s=s: e.tensor_tensor(out=tmps[s], in0=aoT[:, hd, :], in1=rstd_t, op=ALU.mult),
                   reads=[b_ao[hd], b_rstd], writes=[b_tmp[s]])
            act.op(lambda e, hd=hd, s=s: e.activation(out=hT[:, hd, :], in_=tmps[s], func=AF.Identity, scale=vecT[:, hd, bgr:bgr + 1]),
                   reads=[b_tmp[s], b_const], writes=[b_hT[hd]])
        stats_rstd([(ftv[:, g, :], [b_ft]) for g in range(8)], 1024)
        for g in range(8):
            s = g % 4
            eng = dve if g % 3 != 2 else pool
            eng.op(lambda e, g=g, s=s: e.tensor_tensor(out=tmps[s], in0=ftv[:, g, :], in1=rstd_t, op=ALU.mult),
                   reads=[b_ft, b_rstd], writes=[b_tmp[s]])
            act.op(lambda e, g=g, s=s: e.activation(out=hT[:, 8 + g, :], in_=tmps[s], func=AF.Identity, scale=vecT[:, g, bgr + 1:bgr + 2]),
                   reads=[b_tmp[s], b_const], writes=[b_hT[8 + g]])
        if KM < 4:
            return
        for i in range(KC):
            w, bw = wload(Wo[l, i], lambda a: a[:, 0:KC * 128].rearrange("p (k c) -> p k c", k=KC), ("out", l))
            ob = 4 + (i % 2)
            P.mm_group(bank[ob], b_bank[ob], [(w[:, cc, :], hT[:, cc, :]) for cc in range(KC)], reads=[bw], item_reads=b_hT)
            evac_y(i, ob, l, 1)
        postnorm(l, 1)

    def load_x_dma(t):
        t0 = t * T
        sp.dma(xtok, x_d[t0:t0 + T, :].rearrange("(b p) d -> p b d", p=128), writes=[b_xtok] + b_yT, sembuf=b_xtok)

    def load_x_tokmajor(t):
        for c in range(KC):
            pb = c % 4
            for tb in range(4):
                o = bank[pb][:, tb * 128:(tb + 1) * 128]
                i_ = xtok[:, tb, c * 128:(c + 1) * 128]
                pe.op(lambda e, o=o, i_=i_: e.transpose(o, i_, ident32), reads=[b_xtok, b_const], writes=[b_bank[pb]], signal=(tb == 3))
            copy_op(evac_engine(), xT[:, c, :], bank[pb], [b_bank[pb]], [b_xT[c]])

    def load_xT(t):
        t0 = t * T
        sp.dma(xT, XS[:, :, t0:t0 + T].rearrange("c p t -> p c t"), writes=b_xT, sembuf=b_xT[0])

    def store_y(t):
        t0 = t * T
        for tb in range(4):
            for cg in range(4):
                pb = (tb * 4 + cg) % 4
                for k in range(4):
                    c = cg * 4 + k
                    o = bank[pb][:, k * 128:(k + 1) * 128]
                    i_ = xT[:, c, tb * 128:(tb + 1) * 128]
                    pe.op(lambda e, o=o, i_=i_: e.transpose(o, i_, ident32), reads=[b_xT[c], b_const], writes=[b_bank[pb]], signal=(k == 3))
                copy_op(evac_engine(), xtok[:, tb, cg * 512:(cg + 1) * 512], bank[pb], [b_bank[pb]], [b_xtok] + b_yT)
        if int(os.environ.get("KY", "1")):
            sp.dma(y_d[t0:t0 + T, :].rearrange("(b p) d -> p b d", p=128), xtok, reads=[b_xtok], sembuf=b_xtok)

    def fourier(l):
        par = l % 2
        o_ = [0]

        def fc(nbytes):
            a = arena[:, o_[0] // 2:(o_[0] + nbytes) // 2]
            o_[0] += nbytes
            return a
        uTs = [fc(S * 2) for _ in range(2)]
        FTs = [fc(S * 2) for _ in range(2)]
        data1 = fc(2 * 64 * 128 * 2)[0:NA].rearrange("p (b r e) -> p r e b", r=2, e=64)
        data2 = fc(2 * NA * 128 * 2).rearrange("p (e r a) -> p r a e", r=2, a=NA)
        tcp = fc(NA * 128 * 2).rearrange("p (a b) -> p a b", a=NA)
        tsp = fc(NA * 128 * 2).rearrange("p (a b) -> p a b", a=NA)
        wl = [fc(512).bitcast(F32) for _ in range(2)]
        Gb = [fc(512).rearrange("p (r e) -> p r e", r=2) for _ in range(2)]
        assert o_[0] <= main_bytes, (o_[0], main_bytes)
        b_u = [getbuf("fu%d" % i) for i in range(2)]
        b_FT = [getbuf("fF%d" % i) for i in range(2)]
        b_d1, b_d2, b_tt = getbuf("fd1"), getbuf("fd2"), getbuf("ftt")
        b_wl = [getbuf("fwl%d" % i) for i in range(2)]
        b_G = [getbuf("fG%d" % i) for i in range(2)]
        pool.dma(tcp, tcp_d.rearrange("p (a b) -> p a b", a=NA), writes=[b_tt], sembuf=b_tt, max_dma_last_dim=4096)
        pool.dma(tsp, tsp_d.rearrange("p (a b) -> p a b", a=NA), writes=[b_tt], sembuf=b_d1, max_dma_last_dim=4096)
        pbk = [0]

        def nb():
            pbk[0] = (pbk[0] + 1) % 6
            return pbk[0]
        for g in range(NG):
            s = g % 2
            sp.dma(wl[s], fw_d[l, g], writes=[b_wl[s]], sembuf=b_wl[s])
            sp.dma(uTs[s], US[par, g], writes=[b_u[s]], sembuf=b_u[s])
            gbk = 6 + s
            P.mm(bank[gbk][:, 0:128], b_bank[gbk], ccs, wl[s], reads=[b_wl[s], b_const], start=True, stop=True, signal=False)
            P.mm(bank[gbk][:, 128:256], b_bank[gbk], scs, wl[s], reads=[b_wl[s], b_const], start=True, stop=True)
            o = Gb[s]
            i_ = bank[gbk][:, 0:256].rearrange("p (r e) -> p r e", r=2)
            dve.op(lambda e, o=o, i_=i_: e.tensor_copy(out=o, in_=i_), reads=[b_bank[gbk]], writes=[b_G[s]])
            uT = uTs[s]
            KF = int(os.environ.get("KF", "9"))
            for eh in range(2 if KF >= 2 else 0):
                for b0 in range(0, 128, 4):
                    pb = nb()
                    for bi in range(4):
                        b = b0 + bi
                        lhs = uT[:, b::128]
                        rhs = Gb[s][:, :, eh * 64:(eh + 1) * 64]
                        P.mm(bank[pb][0:NA, bi * 128:(bi + 1) * 128], b_bank[pb], lhs, rhs, reads=[b_u[s], b_G[s]],
                             start=True, stop=True, signal=(bi == 3))
                    o = data1[:, :, :, b0:b0 + 4].rearrange("p r e b -> p b (r e)")
                    i_ = bank[pb][0:NA, :].rearrange("p (b x) -> p b x", b=4)
                    copy_op(evac_engine(), o, i_, [b_bank[pb]], [b_d1])
                for e0 in range(0, 64 if KF >= 3 else 0, 2):
                    pb = nb()
                    ne = min(512 // (2 * NA), 2) if 2 * NA * 2 <= 512 else 1
                    for ei in range(2):
                        e_ = e0 + ei
                        oo = bank[pb][:, ei * 2 * NA:(ei + 1) * 2 * NA]
                        P.mm(oo, b_bank[pb], data1[:, 0, e_, :], r1, reads=[b_d1, b_const], start=True, stop=False)
                        P.mm(oo, b_bank[pb], data1[:, 1, e_, :], r2, reads=[b_d1, b_const], start=False, stop=True, signal=(ei == 1))
                    ee = eh * 64 + e0
                    o = data2[:, :, :, ee:ee + 2].rearrange("p r a e -> p e (r a)")
                    i_ = bank[pb][:, 0:2 * 2 * NA].rearrange("p (e x) -> p e x", e=2)
                    copy_op(evac_engine(), o, i_, [b_bank[pb]], [b_d2])
            FT = FTs[s].rearrange("p (b a) -> p a b", a=NA)
            for a0 in range(0, NA if KF >= 4 else 0, 4):
                pb = nb()
                na = min(4, NA - a0)
                for ai in range(na):
                    a_ = a0 + ai
                    oo = bank[pb][:, ai * 128:(ai + 1) * 128]
                    P.mm(oo, b_bank[pb], data2[:, 0, a_, :], tcp[:, a_, :], reads=[b_d2, b_tt], start=True, stop=False)
                    P.mm(oo, b_bank[pb], data2[:, 1, a_, :], tsp[:, a_, :], reads=[b_d2, b_tt], start=False, stop=True, signal=(ai == na - 1))
                o = FT[:, a0:a0 + na, :]
                i_ = bank[pb][:, 0:na * 128].rearrange("p (a b) -> p a b", a=na)
                copy_op(evac_engine(), o, i_, [b_bank[pb]], [b_FT[s]])
            sp.dma(FS[par, g], FTs[s], reads=[b_FT[s]], sembuf=b_FT[s])

    STOP = int(os.environ.get("KSTOP", "99"))
    marks = {}
    MARKS.clear()
    MARKS.update({"_": marks})
    rows = prologue()
    marks["prologue_end"] = pe.cnt
    P.barrier()
    pool_free[0] = False
    if STOP >= 3:
        load_x_dma(0)
    for t in range(NT if STOP >= 3 else 0):
        load_x_tokmajor(t)
        if STOP >= 4:
            ffn(0, 0, 0)
        nxtA = (lambda t=t: load_x_dma(t + 1)) if t + 1 < NT else None
        if STOP >= 5:
            proj(0, t, nxtA)
        elif nxtA:
            nxtA()
        marks["A_t%d" % t] = pe.cnt
        P.barrier()
    pool_free[0] = True
    for l in range(L if STOP >= 6 else 0):
        fourier(l)
        marks["F%d" % l] = pe.cnt
        P.barrier()
        if STOP < 7:
            break
        load_xT(0)
        for t in range(NT):
            mixer(l, t)
            marks["M%d_t%d" % (l, t)] = pe.cnt
            P.barrier()
            if STOP < 8:
                continue
            KB = int(os.environ.get("KB", "9"))
            ffn(l, 1, 2)
            if l + 1 < L:
                nxtB = (lambda t=t: load_xT(t + 1)) if t + 1 < NT else None
                if KB >= 2:
                    ffn(l + 1, 0, 0)
                    proj(l + 1, t, nxtB)
                elif nxtB:
                    nxtB()
            else:
                if KB >= 3:
                    store_y(t)
                if t + 1 < NT:
                    load_xT(t + 1)
            marks["B%d_t%d" % (l, t)] = pe.cnt
            P.barrier()
    P.finalize()
    return nc


_CACHE = {}
MARKS = {}


def run_cores(cfg, per_core_inputs, n_cores):
    key = (cfg.S, cfg.DFF, cfg.L)
    if key not in _CACHE:
        _CACHE[key] = build(cfg)
    nc = _CACHE[key]
    consts = host_constants(cfg)
    in_maps = []
    for ci in range(n_cores):
        m = dict(per_core_inputs[ci])
        m.update(consts)
        in_maps.append(m)
    res = run_bass_kernel_spmd(nc, in_maps, core_ids=list(range(n_cores)))
    if int(os.environ.get("KDBG", "0")):
        return res.results
    return [r["y"] for r in res.results]


def kernel(x_prompt, x_sample, c_prompt, c_sample, w_mod, b_mod, pre_g, post_g,
           ffn_w_gate, ffn_w_up, ffn_w_down, w_in, attn_sink, fourier_w, branch_g, w_out):
    f = lambda a: np.ascontiguousarray(np.asarray(a, dtype=np.float32))
    xs = [f(x_prompt)[0]] + [f(x_sample)[i] for i in range(x_sample.shape[0])]
    cs = [f(c_prompt)[0:1]] + [f(c_sample)[i:i + 1] for i in range(c_sample.shape[0])]
    S = xs[0].shape[0]
    cfg = Cfg(S=S, DFF=ffn_w_gate.shape[-1], L=w_mod.shape[0])
    shared = {"w_mod": f(w_mod), "b_mod": f(b_mod), "pre_g": f(pre_g), "post_g": f(post_g),
              "ffn_w_gate": f(ffn_w_gate), "ffn_w_up": f(ffn_w_up), "ffn_w_down": f(ffn_w_down),
              "w_in": f(w_in), "attn_sink": f(attn_sink), "fourier_w": f(fourier_w),
              "branch_g": f(branch_g), "w_out": f(w_out)}
    n_cores = 8
    per_core = []
    for ci in range(n_cores):
        si = ci if ci < len(xs) else 0
        m = dict(shared)
        m["x"] = xs[si]
        m["c"] = cs[si]
        per_core.append(m)
    ys = run_cores(cfg, per_core, n_cores)
    y_prompt = ys[0][None].astype(np.float32)
    y_sample = np.stack(ys[1:5], axis=0).astype(np.float32)
    return (y_prompt, y_sample)
```

```python
import math
import os
import numpy as np
import concourse.bass as bass
import concourse.mybir as mybir
from concourse.bass_utils import run_bass_kernel_spmd

F32 = mybir.dt.float32
BF16 = mybir.dt.bfloat16
AF = mybir.ActivationFunctionType
ALU = mybir.AluOpType
AX = mybir.AxisListType

D = 2048
KC = 16
T = 512
HD = 128
NQH = 8
NKV = 2
NG = 8
EPS = 1e-6
NEG = -1e30
VROWS = 40


class Tok:
    __slots__ = ("sem", "val", "key")

    def __init__(self, sem, val, key):
        self.sem, self.val, self.key = sem, val, key


class Buf:
    __slots__ = ("name", "w", "rs", "dsem", "dcnt", "excl")

    def __init__(self, name, excl=False):
        self.name = name
        self.excl = excl
        self.w = {}
        self.rs = {}
        self.dsem = None
        self.dcnt = 0


class Eng:
    def __init__(self, prog, name):
        self.prog = prog
        self.name = name
        self.ops = []
        self.sem = prog.nc.alloc_semaphore(name="e_" + name)
        self.key = "e_" + name
        self.cnt = 0
        self.seen = {}
        self.pend_r, self.pend_w = [], []

    def wait(self, tok):
        if tok is None:
            return
        if self.name == "pe" and tok.key == self.key:
            return
        if self.seen.get(tok.key, 0) >= tok.val:
            return
        self.seen[tok.key] = tok.val
        sem, val = tok.sem, tok.val
        self.ops.append(lambda e: e.wait_ge(sem, val))

    def deps(self, reads, writes):
        for b in reads:
            for w_ in b.w.values():
                self.wait(w_)
            if b.excl:
                for r in b.rs.values():
                    self.wait(r)
        for b in writes:
            for w_ in b.w.values():
                self.wait(w_)
            for r in b.rs.values():
                self.wait(r)

    def record(self, tok, reads, writes):
        for b in reads:
            if b.excl:
                b.w[tok.key] = tok
                b.rs = {}
                continue
            o = b.rs.get(tok.key)
            if o is None or o.val < tok.val:
                b.rs[tok.key] = tok
        for b in writes:
            b.w[tok.key] = tok
            b.rs = {}

    def op(self, fn, reads=(), writes=(), signal=True):
        self.deps(reads, writes)
        if not signal:
            self.ops.append(fn)
            self.pend_r.extend(reads)
            self.pend_w.extend(writes)
            return None
        self.cnt += 1
        sem = self.sem
        self.ops.append(lambda e: fn(e).then_inc(sem, 1))
        tok = Tok(sem, self.cnt, self.key)
        self.record(tok, list(reads) + self.pend_r, list(writes) + self.pend_w)
        self.pend_r, self.pend_w = [], []
        self.prog.last[self.key] = tok
        return tok

    def dma(self, out_ap, in_ap, reads=(), writes=(), sembuf=None, track_last=True, **kw):
        self.deps(reads, writes)
        sb = sembuf
        if sb.dsem is None:
            sb.dsem = self.prog.nc.alloc_semaphore(name="d_" + sb.name)
        key = "d_" + sb.name
        if sb.dcnt > 0:
            self.wait(Tok(sb.dsem, sb.dcnt, key))
        sb.dcnt += 16
        sem, val = sb.dsem, sb.dcnt
        self.ops.append(lambda e: e.dma_start(out=out_ap, in_=in_ap, **kw).then_inc(sem, 16))
        tok = Tok(sem, val, key)
        self.record(tok, reads, writes)
        if track_last:
            self.prog.last[key] = tok
        return tok


class Prog:
    def __init__(self):
        self.nc = bass.Bass("TRN2", target_bir_lowering=False)
        self.last = {}
        self.pe = Eng(self, "pe")
        self.act = Eng(self, "act")
        self.dve = Eng(self, "dve")
        self.pool = Eng(self, "pool")
        self.sp = Eng(self, "sp")
        self.engs = [self.pe, self.act, self.dve, self.pool, self.sp]

    def mm(self, out_ap, out_buf, lhsT, rhs, reads, start, stop, signal=None, **kw):
        pe = self.pe
        if signal is None:
            signal = stop
        if start:
            pe.deps(reads, [out_buf])
        else:
            pe.deps(reads, [])
        if signal:
            pe.cnt += 1
            sem = pe.sem
            pe.ops.append(lambda e: e.matmul(out_ap, lhsT, rhs, start=start, stop=stop, **kw).then_inc(sem, 1))
            tok = Tok(sem, pe.cnt, pe.key)
            pe.record(tok, list(reads) + pe.pend_r, [out_buf] + pe.pend_w)
            pe.pend_r, pe.pend_w = [], []
            self.last[pe.key] = tok
            return tok
        pe.ops.append(lambda e: e.matmul(out_ap, lhsT, rhs, start=start, stop=stop, **kw))
        pe.pend_r.extend(reads)
        pe.pend_w.append(out_buf)
        return None

    def mm_group(self, out_ap, out_buf, items, reads, item_reads=None):
        pe = self.pe
        pe.deps(reads, [out_buf])
        n = len(items)
        if item_reads is not None:
            reads = list(reads) + list(item_reads)
        for i, (l, r) in enumerate(items):
            st, last = (i == 0), (i == n - 1)
            if item_reads is not None:
                pe.deps([item_reads[i]], [])
            if last:
                pe.cnt += 1
                sem = pe.sem
                pe.ops.append(lambda e, l=l, r=r, st=st: e.matmul(out_ap, l, r, start=st, stop=True).then_inc(sem, 1))
            else:
                pe.ops.append(lambda e, l=l, r=r, st=st: e.matmul(out_ap, l, r, start=st, stop=False))
        tok = Tok(pe.sem, pe.cnt, pe.key)
        pe.record(tok, list(reads) + pe.pend_r, [out_buf] + pe.pend_w)
        pe.pend_r, pe.pend_w = [], []
        self.last[pe.key] = tok
        return tok

    def barrier(self):
        toks = list(self.last.values())
        for e in self.engs:
            for t in toks:
                e.wait(t)

    def finalize(self):
        nc = self.nc
        for t in list(self.last.values()):
            self.sp.wait(t)
        with nc.allow_non_contiguous_dma(reason="small vector / layout loads"):
            with nc.Block() as block:
                def mk(engw):
                    def body(e):
                        for f in engw.ops:
                            f(e)
                    return body
                block.tensor(mk(self.pe))
                block.scalar(mk(self.act))
                block.vector(mk(self.dve))
                block.gpsimd(mk(self.pool))
                block.sync(mk(self.sp))
        return nc


class Cfg:
    def __init__(self, S=8192, DFF=5632, L=2):
        self.S, self.DFF, self.L = S, DFF, L
        self.NT = S // T
        self.NJ = DFF // 128
        self.NA = S // 128
        self.NBLK = S // 128


def host_constants(cfg):
    S, NA = cfg.S, cfg.NA
    inv_freq = (10000.0 ** (-np.arange(0, HD, 2, dtype=np.float32) / HD)).astype(np.float32)
    ang = np.arange(S, dtype=np.float32)[:, None] * inv_freq[None, :]
    cos, sin = np.cos(ang).astype(np.float32), np.sin(ang).astype(np.float32)
    ropeC = np.concatenate([cos, cos], axis=1).T.copy()
    ropeS = np.concatenate([sin, sin], axis=1).T.copy()
    pm = np.zeros((128, 128), np.float32)
    for m in range(64):
        pm[m + 64, m] = -1.0
    for m in range(64, 128):
        pm[m - 64, m] = 1.0
    i = np.arange(128)[:, None]
    j = np.arange(384)[None, :]
    mask = np.where((j >= i) & (j <= i + 256), 0.0, NEG).astype(np.float32)
    c = np.arange(128)
    scale = 1.0 / math.sqrt(S * 128.0)
    angc = 2.0 * np.pi * ((c[:, None] * c[None, :]) % 128) / 128.0
    ccs = (np.cos(angc) * scale).astype(np.float32)
    scs = (-np.sin(angc) * scale).astype(np.float32)
    a = np.arange(NA)
    anga = 2.0 * np.pi * ((a[:, None] * a[None, :]) % NA) / NA
    ca, sa = np.cos(anga), np.sin(anga)
    r1 = np.concatenate([ca, -sa], axis=1).astype(np.float32)
    r2 = np.concatenate([sa, ca], axis=1).astype(np.float32)
    b = np.arange(128)
    sp = np.arange(S)
    angt = 2.0 * np.pi * ((b[:, None].astype(np.int64) * sp[None, :].astype(np.int64)) % S) / S
    tc = np.cos(angt).reshape(128, 128, NA).transpose(0, 2, 1)
    ts = np.sin(angt).reshape(128, 128, NA).transpose(0, 2, 1)
    return {
        "ropeC": ropeC, "ropeS": ropeS, "pmat": pm, "ident": np.eye(128, dtype=np.float32),
        "ones": np.ones((128, 128), np.float32), "maskb": mask, "ccs": ccs, "scs": scs,
        "r1": r1, "r2": r2,
        "tcp": np.ascontiguousarray(tc).astype(np.float32).reshape(128, NA * 128),
        "tsp": np.ascontiguousarray(ts).astype(np.float32).reshape(128, NA * 128),
    }


def build(cfg):
    S, DFF, L, NT, NJ, NA = cfg.S, cfg.DFF, cfg.L, cfg.NT, cfg.NJ, cfg.NA
    P = Prog()
    nc = P.nc
    pe, act, dve, pool, sp = P.pe, P.act, P.dve, P.pool, P.sp

    def din(name, shape):
        return nc.dram_tensor(name, list(shape), F32, kind="ExternalInput").ap()

    x_d = din("x", [S, D])
    c_d = din("c", [1, D])
    wmod_d = din("w_mod", [L, D, 9 * D])
    bmod_d = din("b_mod", [L, 9 * D])
    preg_d = din("pre_g", [L, 3, D])
    postg_d = din("post_g", [L, 3, D])
    wg_d = din("ffn_w_gate", [L, 2, D, DFF])
    wu_d = din("ffn_w_up", [L, 2, D, DFF])
    wd_d = din("ffn_w_down", [L, 2, DFF, D])
    win_d = din("w_in", [L, D, 2560])
    sink_d = din("attn_sink", [L, 8])
    fw_d = din("fourier_w", [L, 8, 128, 128])
    bg_d = din("branch_g", [L, 2, 1024])
    wo_d = din("w_out", [L, D, D])
    ropeC_d = din("ropeC", [128, S])
    ropeS_d = din("ropeS", [128, S])
    pmat_d = din("pmat", [128, 128])
    ident_d = din("ident", [128, 128])
    ones_d = din("ones", [128, 128])
    maskb_d = din("maskb", [128, 384])
    ccs_d = din("ccs", [128, 128])
    scs_d = din("scs", [128, 128])
    r1_d = din("r1", [NA, 2 * NA])
    r2_d = din("r2", [NA, 2 * NA])
    tcp_d = din("tcp", [128, NA * 128])
    tsp_d = din("tsp", [128, NA * 128])
    y_d = nc.dram_tensor("y", [S, D], F32, kind="ExternalOutput").ap()

    DBG = bool(int(os.environ.get("KDBG", "0")))

    def dscr(name, shape, dt=BF16):
        return nc.dram_tensor(name, list(shape), dt, kind=("ExternalOutput" if DBG and name in ("QS", "KS", "VS", "US", "FS", "XS") else "Internal")).ap()

    Wgu = dscr("Wgu", [L, 2, NJ, 128, 2, KC, 128])
    Wd = dscr("Wd", [L, 2, 16, 128, NJ, 128])
    Win = dscr("Win", [L, 18, 128, KC, 128])
    Wv = dscr("Wv", [L, 128, KC, 256])
    Wo = dscr("Wo", [L, 16, 128, KC, 128])
    QS = dscr("QS", [2, 8, 128, S])
    KS = dscr("KS", [2, 2, 128, S])
    VS = dscr("VS", [2, S, 256])
    US = dscr("US", [2, 8, 128, S])
    FS = dscr("FS", [2, 8, 128, S])
    XS = dscr("XS", [KC, 128, S], F32)

    def sb(name, shape, dt):
        return nc.alloc_sbuf_tensor("s_" + name, list(shape), dt).ap()

    ident32 = sb("ident32", [128, 128], F32)
    pmat = sb("pmat", [128, 128], F32)
    ccs = sb("ccs", [128, 128], F32)
    scs = sb("scs", [128, 128], F32)
    identb = sb("identb", [128, 128], BF16)
    onesb = sb("onesb", [128, 128], BF16)
    maskb = sb("maskb", [128, 384], BF16)
    r1 = sb("r1", [NA, 2 * NA], BF16)
    r2 = sb("r2", [NA, 2 * NA], BF16)
    epst = sb("epst", [128, 1], F32)
    vecT = sb("vecT", [128, KC, VROWS], F32)
    cact = sb("cact", [128, KC, 2], F32)
    modT = sb("modT", [128, L, 9, KC], F32)
    gsT = sb("gsT", [128, L, 3, KC], F32)
    coefT = sb("coefT", [128, L, 3, KC], F32)
    sinkb = sb("sinkb", [128, L, 8], F32)
    nsinkb = sb("nsinkb", [128, L, 8], F32)
    small = sb("small", [128, 64], F32)
    diag2 = sb("diag2", [128, 8, 128], BF16)
    b_const = Buf("const")
    b_small = [Buf("sm%d" % i) for i in range(8)]
    b_diag2 = [Buf("diag%d" % i) for i in range(8)]
    b_sm2 = [[Buf("smx%d_%d" % (p_, i)) for i in range(7)] for p_ in range(2)]
    b_ps = [Buf("pA"), Buf("pB")]

    XT_B, HT_B, ACT_B, YT_B = 32768, 16384, NJ * 1024 if NJ * 1024 > 45056 else 45056, 32768
    WSLOT = max(2 * KC * 128 * 2, NJ * 128 * 2, KC * 256 * 2)
    NW = 3
    SCR_B = 4 * 1024 + 4 * 2048 + 2048 + 4096
    tot = XT_B + HT_B + ACT_B + YT_B + NW * WSLOT + SCR_B
    arena = nc.alloc_sbuf_tensor("arena", [128, tot // 2], BF16).ap()
    off = [0]

    def carve(nbytes):
        a = arena[:, off[0] // 2:(off[0] + nbytes) // 2]
        off[0] += nbytes
        return a

    xT = carve(XT_B).bitcast(F32).rearrange("p (c t) -> p c t", c=KC)
    hT = carve(HT_B).rearrange("p (c t) -> p c t", c=KC)
    act_raw = carve(ACT_B)
    yT_raw = carve(YT_B)
    yT = yT_raw.bitcast(F32).rearrange("p (c t) -> p c t", c=KC)
    wslots = [carve(WSLOT) for _ in range(NW)]
    sqs = [carve(1024) for _ in range(4)]
    tmps = [carve(2048).bitcast(F32) for _ in range(4)]
    rstd_t = carve(2048).bitcast(F32)
    sd_t = rstd_t
    ropeCt = carve(2048).bitcast(F32)
    ropeSt = carve(2048).bitcast(F32)
    main_bytes = off[0]

    actc = act_raw[:, 0:NJ * 512].rearrange("p (j t) -> p j t", j=NJ)
    qv = act_raw[:, 0:8 * 512].rearrange("p (h t) -> p h t", h=8)
    kv = act_raw[:, 8 * 512:8 * 512 + 2 * 768].rearrange("p (g t) -> p g t", g=2)
    vv = act_raw[:, 11 * 512:11 * 512 + 6 * 256].rearrange("p (b c) -> p b c", b=6)
    pv = act_raw[:, 14 * 512:14 * 512 + 4 * 384].rearrange("p (h k) -> p h k", h=4)
    ptv = act_raw[:, 17 * 512:17 * 512 + 3 * 512].rearrange("p (k t) -> p k t", k=3)
    pvs = [pv, yT_raw[:, 0:4 * 384].rearrange("p (h k) -> p h k", h=4)]
    aoT = act_raw[:, 20 * 512:36 * 512].bitcast(F32).rearrange("p (h t) -> p h t", h=8)
    ftv = act_raw[:, 36 * 512:44 * 512].rearrange("p (g t) -> p g t", g=8)
    qst = act_raw[:, 0:10 * 512].rearrange("p (h t) -> p h t", h=10)
    ust = act_raw[:, 10 * 512:18 * 512].rearrange("p (g t) -> p g t", g=8)
    vst = act_raw[:, 18 * 512:18 * 512 + 4 * 256].rearrange("p (b c) -> p b c", b=4)
    qf_t = act_raw[:, 20 * 512:20 * 512 + 2 * 1024].bitcast(F32).rearrange("p (k t) -> p k t", k=2)
    xtok = yT_raw.bitcast(F32).rearrange("p (b d) -> p b d", b=4)

    PS = nc.alloc_psum_tensor("PS", [128, 8 * 512], F32).ap()
    bank = [PS[:, i * 512:(i + 1) * 512] for i in range(8)]
    b_bank = [Buf("bank%d" % i, excl=True) for i in range(8)]

    b_xT = [Buf("xT%d" % i) for i in range(KC)]
    b_hT = [Buf("hT%d" % i) for i in range(KC)]
    b_yT = [Buf("yT%d" % i) for i in range(KC)]
    b_act = [Buf("act%d" % i) for i in range(max(NJ, 44))]
    b_w = [Buf("w%d" % i) for i in range(NW)]
    b_sq = [Buf("sq%d" % i) for i in range(4)]
    b_tmp = [Buf("tmp%d" % i) for i in range(4)]
    b_rstd = Buf("rstd")
    b_sd = Buf("sd")
    b_rope = Buf("rope")
    b_q, b_k, b_v, b_ft = Buf("q"), Buf("k"), Buf("v"), Buf("ft")
    b_p = Buf("p")
    b_pt = [Buf("pt%d" % i) for i in range(3)]
    b_ao = [Buf("ao%d" % i) for i in range(8)]
    b_qst = [Buf("qst%d" % i) for i in range(10)]
    b_ust = [Buf("ust%d" % i) for i in range(8)]
    b_vst = Buf("vst")
    b_qf = [Buf("qf%d" % i) for i in range(2)]
    b_xtok = Buf("xtok")
    b_dram = Buf("dram")

    cnt = {"w": 0, "ev": 0}
    bufcache = {}

    def getbuf(name):
        if name not in bufcache:
            bufcache[name] = Buf(name)
        return bufcache[name]

    conv_toks = {}
    pool_free = [False]

    def wload(src_ap, view_fn, grp):
        s = cnt["w"] % NW
        cnt["w"] += 1
        v = view_fn(wslots[s])
        for t_ in conv_toks.get(grp, ()):
            sp.wait(t_)
        sp.dma(v, src_ap, writes=[b_w[s]], sembuf=b_w[s])
        return v, b_w[s]

    def evac_engine():
        cnt["ev"] += 1
        return act if cnt["ev"] % 2 == 0 else dve

    def copy_op(eng, out, in_, reads, writes):
        if eng is act:
            return act.op(lambda e: e.activation(out=out, in_=in_, func=AF.Copy), reads, writes)
        return eng.op(lambda e: e.tensor_copy(out=out, in_=in_), reads, writes)

    def prologue():
        k = 0
        for dst, src in ((ident32, ident_d), (pmat, pmat_d), (ccs, ccs_d), (scs, scs_d)):
            sp.dma(dst, src, writes=[b_const], sembuf=b_small[k % 8]); k += 1
        for dst, src in ((identb, ident_d), (onesb, ones_d), (maskb, maskb_d), (r1, r1_d), (r2, r2_d)):
            pool.dma(dst, src, writes=[b_const], sembuf=b_small[k % 8]); k += 1
        sp.dma(sinkb.rearrange("p l h -> p (l h)"),
               sink_d.rearrange("(o l) h -> o (l h)", o=1).broadcast_to([128, L * 8]),
               writes=[b_const], sembuf=b_small[k % 8]); k += 1
        V = yT_raw.bitcast(F32)[0:VROWS, 0:D]
        pool.op(lambda e: e.memset(yT_raw.bitcast(F32)[0:64, 0:D], 0.0), writes=[b_xtok])
        pool.op(lambda e: e.memset(epst, EPS), writes=[b_const])
        if STOP >= 2:
            convert_weights()
        r = 0
        rows = {}
        for name, src, n in (("pre", preg_d.rearrange("l j d -> (l j) d"), 3 * L),
                             ("post", postg_d.rearrange("l j d -> (l j) d"), 3 * L),
                             ("bmod", bmod_d.rearrange("l (m d) -> (l m) d", d=D), 9 * L),
                             ("c", c_d, 1)):
            rows[name] = r
            sp.dma(V[r:r + n, :], src, writes=[b_xtok], sembuf=b_small[k % 8]); k += 1
            r += n
        rows["bg"] = r
        sp.dma(V[r:r + 2 * L, 0:1024], bg_d.rearrange("l b d -> (l b) d"), writes=[b_xtok], sembuf=b_small[k % 8]); k += 1
        r += 2 * L
        assert r <= VROWS
        for half in range(2):
            for cc in range(8):
                c = half * 8 + cc
                o = bank[half][:, cc * VROWS:(cc + 1) * VROWS]
                i_ = V[:, c * 128:(c + 1) * 128]
                pe.op(lambda e, o=o, i_=i_: e.transpose(o, i_, ident32[0:VROWS, 0:VROWS]),
                      reads=[b_xtok, b_const], writes=[b_bank[half]])
            o = vecT[:, half * 8:(half + 1) * 8, :]
            i_ = bank[half][:, 0:8 * VROWS].rearrange("p (c r) -> p c r", c=8)
            dve.op(lambda e, o=o, i_=i_: e.tensor_copy(out=o, in_=i_), reads=[b_bank[half]], writes=[b_const])
        for dup in range(2):
            o = cact[:, :, dup]
            i_ = vecT[:, :, rows["c"]]
            act.op(lambda e, o=o, i_=i_: e.activation(out=o, in_=i_, func=AF.Silu), reads=[b_const], writes=[b_const])
        dve.op(lambda e: e.tensor_scalar(out=nsinkb, in0=sinkb, scalar1=-1.0, scalar2=None, op0=ALU.mult),
               reads=[b_const], writes=[b_const])
        OCB = 4
        stg = [arena[:, 0:KC * OCB * 128 * 2].bitcast(F32).rearrange("p (k c) -> p k c", k=KC),
               act_raw[:, 0:KC * OCB * 128 * 2].bitcast(F32).rearrange("p (k c) -> p k c", k=KC)]
        b_stg = [b_act[0], b_act[1]]
        nblk = 144 // OCB
        for l in range(L):
            mb = bank[2 + (l % 2)]
            for blk in range(nblk):
                s = (l * nblk + blk) % 2
                src = wmod_d[l, :, blk * OCB * 128:(blk + 1) * OCB * 128].rearrange("(k p) c -> p k c", p=128)
                sp.dma(stg[s], src, writes=[b_stg[s]], sembuf=b_stg[s])
                for oc in range(OCB):
                    col = (blk * OCB + oc) * 2
                    for kc in range(KC):
                        P.mm(mb[:, col:col + 2], b_bank[2 + (l % 2)], stg[s][:, kc, oc * 128:(oc + 1) * 128], cact[:, kc, :],
                             reads=[b_stg[s], b_const], start=(kc == 0), stop=(kc == KC - 1),
                             signal=(kc == KC - 1 and oc == OCB - 1))
            i0 = mb[:, 0:288].rearrange("p (jm c two) -> p jm c two", jm=9, c=KC)[:, :, :, 0]
            i1 = vecT[:, :, rows["bmod"] + l * 9:rows["bmod"] + (l + 1) * 9].rearrange("p c jm -> p jm c")
            o = modT[:, l]
            dve.op(lambda e, o=o, i0=i0, i1=i1: e.tensor_tensor(out=o, in0=i0, in1=i1, op=ALU.add),
                   reads=[b_bank[2 + (l % 2)], b_const], writes=[b_const])
            for j in range(3):
                wgt = (0.5, 1.0, 0.5)[j]
                o = gsT[:, l, j, :]
                sc = modT[:, l, 3 * j + 1, :]
                pg = vecT[:, :, rows["pre"] + l * 3 + j]
                dve.op(lambda e, o=o, sc=sc, pg=pg: e.scalar_tensor_tensor(out=o, in0=sc, scalar=1.0, in1=pg, op0=ALU.add, op1=ALU.mult),
                       reads=[b_const], writes=[b_const])
                o2 = coefT[:, l, j, :]
                gt = modT[:, l, 3 * j + 2, :]
                qg = vecT[:, :, rows["post"] + l * 3 + j]
                dve.op(lambda e, o2=o2, gt=gt, qg=qg: e.scalar_tensor_tensor(out=o2, in0=gt, scalar=1.0, in1=qg, op0=ALU.add, op1=ALU.mult),
                       reads=[b_const], writes=[b_const])
                if wgt != 1.0:
                    dve.op(lambda e, o2=o2, wgt=wgt: e.tensor_scalar(out=o2, in0=o2, scalar1=wgt, scalar2=None, op0=ALU.mult),
                           reads=[b_const], writes=[b_const])
        return rows

    def convert_weights():
        cvb = [Buf("cv%d" % i) for i in range(6)]
        k = [0]
        cur = [None]

        def cv(out_ap, in_ap):
            t_ = pool.dma(out_ap, in_ap, sembuf=cvb[k[0] % 6], track_last=False, max_dma_last_dim=4096)
            conv_toks.setdefault(cur[0], {})[t_.key] = t_
            k[0] += 1

        def conv_ffn(l, f):
            cur[0] = ("ffn", l, f)
            for m, src in ((0, wg_d), (1, wu_d)):
                v = src[l, f].rearrange("(kc p) (j c) -> kc j p c", p=128, c=128)
                for kc in range(KC):
                    cv(Wgu[l, f, :, :, m, kc, :], v[kc])
            v = wd_d[l, f].rearrange("(jc p) (i c) -> jc i p c", p=128, c=128)
            for jc in range(NJ):
                cv(Wd[l, f, :, :, jc, :], v[jc])

        def conv_in(l):
            cur[0] = ("in", l)
            v = win_d[l].rearrange("(kc p) (j c) -> kc j p c", p=128, c=128)
            for kc in range(KC):
                cv(Win[l, 0:10, :, kc, :], v[kc, 0:10])
                cv(Win[l, 10:18, :, kc, :], v[kc, 12:20])
            v2 = win_d[l].rearrange("(kc p) c -> kc p c", p=128)
            for kc in range(KC):
                cv(Wv[l, :, kc, :], v2[kc, :, 1280:1536])

        def conv_out(l):
            cur[0] = ("out", l)
            v = wo_d[l].rearrange("(kc p) (j c) -> kc j p c", p=128, c=128)
            for kc in range(KC):
                cv(Wo[l, :, :, kc, :], v[kc])

        conv_ffn(0, 0)
        conv_in(0)
        for l in range(L):
            conv_out(l)
            conv_ffn(l, 1)
            if l + 1 < L:
                conv_ffn(l + 1, 0)
                conv_in(l + 1)
        for g_ in list(conv_toks):
            conv_toks[g_] = list(conv_toks[g_].values())

    def stats_rstd(srcs, dim):
        n = len(srcs)
        for i, (ap, bufs) in enumerate(srcs):
            s = i % 4
            if i % 3 == 2 and pool_free[0]:
                pool.op(lambda e, ap=ap, s=s: e.tensor_tensor(out=sqs[s], in0=ap, in1=ap, op=ALU.mult), reads=bufs, writes=[b_sq[s]])
            else:
                act.op(lambda e, ap=ap, s=s: e.activation(out=sqs[s], in_=ap, func=AF.Square), reads=bufs, writes=[b_sq[s]])
            P.mm(bank[6], b_bank[6], onesb, sqs[s], reads=[b_sq[s], b_const], start=(i == 0), stop=(i == n - 1), signal=True)
        finish_rstd(dim)

    def finish_rstd(dim):
        if int(os.environ.get("KRSQ", "0")):
            act.op(lambda e: e.activation(out=rstd_t, in_=bank[6], func=AF.Abs_reciprocal_sqrt, bias=epst, scale=1.0 / dim),
                   reads=[b_bank[6], b_const], writes=[b_rstd])
        else:
            act.op(lambda e: e.activation(out=rstd_t, in_=bank[6], func=AF.Sqrt, bias=epst, scale=1.0 / dim),
                   reads=[b_bank[6], b_const], writes=[b_rstd])
            dve.op(lambda e: e.reciprocal(out=rstd_t, in_=rstd_t), reads=[], writes=[b_rstd])

    def prenorm(l, j):
        stats_rstd([(xT[:, c, :], [b_xT[c]]) for c in range(KC)], D)
        for c in range(KC):
            s = c % 4
            eng = dve if (c % 3 != 2 or not pool_free[0]) else pool
            eng.op(lambda e, c=c, s=s: e.tensor_tensor(out=tmps[s], in0=xT[:, c, :], in1=rstd_t, op=ALU.mult),
                   reads=[b_xT[c], b_rstd], writes=[b_tmp[s]])
            act.op(lambda e, c=c, s=s: e.activation(out=hT[:, c, :], in_=tmps[s], func=AF.Identity,
                                                    scale=gsT[:, l, j, c:c + 1], bias=modT[:, l, 3 * j, c:c + 1]),
                   reads=[b_tmp[s], b_const], writes=[b_hT[c]])

    def postnorm(l, j):
        flush_stats()
        finish_rstd(D)
        for c in range(KC):
            s = c % 4
            e1 = pool if (pool_free[0] and c % 3 == 2) else dve
            e2 = pool if (pool_free[0] and c % 3 == 0) else dve
            e1.op(lambda e, c=c, s=s: e.tensor_tensor(out=tmps[s], in0=yT[:, c, :], in1=rstd_t, op=ALU.mult),
                  reads=[b_yT[c], b_rstd], writes=[b_tmp[s]])
            e2.op(lambda e, c=c, s=s: e.tensor_tensor(out=xT[:, c, :], in0=xT[:, c, :], in1=tmps[s], op=ALU.add),
                  reads=[b_tmp[s]], writes=[b_xT[c]])

    pend_stats = []

    def flush_stats():
        while pend_stats:
            s_, i_ = pend_stats.pop(0)
            P.mm(bank[6], b_bank[6], onesb, sqs[s_], reads=[b_sq[s_], b_const], start=(i_ == 0), stop=(i_ == KC - 1), signal=True)

    def evac_y(i, ob, l, j):
        s = i % 4
        flush_stats()
        act.op(lambda e, s=s, ob=ob: e.activation(out=sqs[s], in_=bank[ob], func=AF.Square), reads=[b_bank[ob]], writes=[b_sq[s]])
        act.op(lambda e, i=i, ob=ob: e.activation(out=yT[:, i, :], in_=bank[ob], func=AF.Identity, scale=coefT[:, l, j, i:i + 1]),
               reads=[b_bank[ob], b_const], writes=[b_yT[i]])
        pend_stats.append((s, i))

    def ffn(l, f, j):
        FS_ = int(os.environ.get("KFFN", "9"))
        prenorm(l, j)
        if FS_ < 2:
            return
        for jj in range(NJ):
            w, bw = wload(Wgu[l, f, jj], lambda a: a[:, 0:2 * KC * 128].rearrange("p (m k c) -> p m k c", m=2, k=KC), ("ffn", l, f))
            st = jj % 2
            gb, ub = 2 * st, 2 * st + 1
            P.mm_group(bank[gb], b_bank[gb], [(w[:, 0, kc, :], hT[:, kc, :]) for kc in range(KC)], reads=[bw], item_reads=b_hT)
            P.mm_group(bank[ub], b_bank[ub], [(w[:, 1, kc, :], hT[:, kc, :]) for kc in range(KC)], reads=[bw], item_reads=b_hT)
            act.op(lambda e, st=st, gb=gb: e.activation(out=tmps[st], in_=bank[gb], func=AF.Silu), reads=[b_bank[gb]], writes=[b_tmp[st]])
            dve.op(lambda e, st=st, ub=ub, jj=jj: e.tensor_tensor(out=actc[:, jj, :], in0=tmps[st], in1=bank[ub], op=ALU.mult),
                   reads=[b_tmp[st], b_bank[ub]], writes=[b_act[jj]])
        if FS_ < 3:
            return
        for i in range(KC):
            w, bw = wload(Wd[l, f, i], lambda a: a[:, 0:NJ * 128].rearrange("p (j c) -> p j c", j=NJ), ("ffn", l, f))
            ob = 4 + (i % 2)
            P.mm_group(bank[ob], b_bank[ob], [(w[:, jc, :], actc[:, jc, :]) for jc in range(NJ)], reads=[bw], item_reads=b_act[:NJ])
            evac_y(i, ob, l, j)
        if FS_ < 4:
            return
        postnorm(l, j)

    def proj(l, t, nxt=None):
        t0 = t * T
        par = l % 2
        prenorm(l, 1)
        sp.dma(XS[:, :, t0:t0 + T].rearrange("c p t -> p c t"), xT, reads=b_xT, sembuf=b_xT[0])
        if nxt is not None:
            nxt()
        sp.dma(ropeCt, ropeC_d[:, t0:t0 + T], writes=[b_rope], sembuf=b_rope)
        sp.dma(ropeSt, ropeS_d[:, t0:t0 + T], writes=[b_rope], sembuf=b_small[0])
        for jq in range(18):
            w, bw = wload(Win[l, jq], lambda a: a[:, 0:KC * 128].rearrange("p (k c) -> p k c", k=KC), ("in", l))
            pb = jq % 4
            P.mm_group(bank[pb], b_bank[pb], [(w[:, kc, :], hT[:, kc, :]) for kc in range(KC)], reads=[bw], item_reads=b_hT)
            if jq < 10:
                s = jq % 2
                act.op(lambda e, s=s, pb=pb: e.activation(out=qf_t[:, s, :], in_=bank[pb], func=AF.Copy), reads=[b_bank[pb]], writes=[b_qf[s]])
                P.mm(bank[7], b_bank[7], pmat, qf_t[:, s, :], reads=[b_qf[s], b_const], start=True, stop=True)
                dve.op(lambda e, s=s: e.tensor_tensor(out=tmps[s], in0=bank[7], in1=ropeSt, op=ALU.mult),
                       reads=[b_bank[7], b_rope], writes=[b_tmp[s]])
                pe_ = pool if pool_free[0] else dve
                pe_.op(lambda e, s=s: e.tensor_tensor(out=qf_t[:, s, :], in0=qf_t[:, s, :], in1=ropeCt, op=ALU.mult),
                       reads=[b_rope], writes=[b_qf[s]])
                pe_.op(lambda e, s=s, jq=jq: e.tensor_tensor(out=qst[:, jq, :], in0=qf_t[:, s, :], in1=tmps[s], op=ALU.add),
                       reads=[b_qf[s], b_tmp[s]], writes=[b_qst[jq]])
            else:
                g = jq - 10
                copy_op(evac_engine(), ust[:, g, :], bank[pb], [b_bank[pb]], [b_ust[g]])
        sp.dma(QS[par, :, :, t0:t0 + T].rearrange("h d t -> d h t"), qst[:, 0:8, :], reads=b_qst[0:8], sembuf=b_qst[0])
        sp.dma(KS[par, :, :, t0:t0 + T].rearrange("h d t -> d h t"), qst[:, 8:10, :], reads=b_qst[8:10], sembuf=b_qst[8])
        sp.dma(US[par, :, :, t0:t0 + T].rearrange("g c t -> c g t"), ust, reads=b_ust, sembuf=b_ust[0])
        w, bw = wload(Wv[l], lambda a: a[:, 0:KC * 256].rearrange("p (k c) -> p k c", k=KC), ("in", l))
        for tb in range(4):
            pb = tb % 4
            P.mm_group(bank[pb][:, 0:256], b_bank[pb], [(hT[:, kc, tb * 128:(tb + 1) * 128], w[:, kc, :]) for kc in range(KC)],
                       reads=[bw] + b_hT)
            copy_op(evac_engine(), vst[:, tb, :], bank[pb][:, 0:256], [b_bank[pb]], [b_vst])
        sp.dma(VS[par, t0:t0 + T, :].rearrange("(b p) c -> p b c", p=128), vst, reads=[b_vst], sembuf=b_vst)

    def mixer(l, t):
        t0 = t * T
        par = l % 2
        scale = HD ** -0.5
        sp.dma(qv, QS[par, :, :, t0:t0 + T].rearrange("h d t -> d h t"), writes=[b_q], sembuf=b_q)
        lo, hi = max(t0 - 128, 0), min(t0 + T + 128, S)
        klo = lo - (t0 - 128)
        sp.dma(kv[:, :, klo:klo + (hi - lo)], KS[par, :, :, lo:hi].rearrange("g d t -> d g t"), writes=[b_k], sembuf=b_k)
        blo = klo // 128
        nb = (hi - lo) // 128
        sp.dma(vv[:, blo:blo + nb, :], VS[par, lo:hi, :].rearrange("(b p) c -> p b c", p=128), writes=[b_v], sembuf=b_v)
        sp.dma(ftv, FS[par, :, :, t0:t0 + T].rearrange("g e t -> e g t"), writes=[b_ft], sembuf=b_ft)
        KM = int(os.environ.get("KM", "9"))
        if KM < 1:
            return
        units = [(qb, g) for qb in range(4 if KM >= 2 else 0) for g in range(2)]

        def geom(qb):
            n = t * 4 + qb
            dl = [d for d in (-1, 0, 1) if 0 <= n + d < S // 128]
            return dl, len(dl), (dl[0] + 1) * 128, (qb + dl[0] + 1) * 128

        def scores_softmax(ui):
            qb, g = units[ui]
            pr = ui % 2
            dl, nkb, mo, kcol0 = geom(qb)
            nk = nkb * 128
            sm = small[:, pr * 32:(pr + 1) * 32]
            mx, negm, sums, esi, es, den, rr = (sm[:, 4 * i:4 * i + 4] for i in range(7))
            bmx, bnegm, bsums, besi, bes, bden, brr = b_sm2[pr]
            pvp = pvs[pr]
            for h in range(4):
                hd = g * 4 + h
                P.mm(bank[h][:, 0:nk], b_bank[h], qv[:, hd, qb * 128:(qb + 1) * 128], kv[:, g, kcol0:kcol0 + nk],
                     reads=[b_q, b_k], start=True, stop=False)
                P.mm(bank[h][:, 0:nk], b_bank[h], identb, maskb[:, mo:mo + nk], reads=[b_const], start=False, stop=True)
            scv = PS[:, 0:4 * 512].rearrange("p (h k) -> p h k", h=4)[:, :, 0:nk]
            dve.op(lambda e: e.tensor_reduce(out=mx, in_=scv, axis=AX.X, op=ALU.max), reads=b_bank[0:4], writes=[bmx])
            ns = nsinkb[:, l, g * 4:(g + 1) * 4]
            dve.op(lambda e: e.scalar_tensor_tensor(out=negm, in0=mx, scalar=-scale, in1=ns, op0=ALU.mult, op1=ALU.min),
                   reads=[bmx, b_const], writes=[bnegm])
            for h in range(4):
                act.op(lambda e, h=h: e.activation(out=pvp[:, h, 0:nk], in_=bank[h][:, 0:nk], func=AF.Exp,
                                                   bias=negm[:, h:h + 1], scale=scale, accum_out=sums[:, h:h + 1]),
                       reads=[b_bank[h], bnegm], writes=[b_ps[pr], bsums])
            for h in range(4):
                hd_ = g * 4 + h
                act.op(lambda e, h=h, hd_=hd_: e.activation(out=es[:, h:h + 1], in_=negm[:, h:h + 1], func=AF.Exp, bias=sinkb[:, l, hd_:hd_ + 1]),
                       reads=[bnegm, b_const], writes=[bes])

        def softmax_tail(ui):
            qb, g = units[ui]
            pr = ui % 2
            sm = small[:, pr * 32:(pr + 1) * 32]
            mx, negm, sums, esi, es, den, rr = (sm[:, 4 * i:4 * i + 4] for i in range(7))
            bmx, bnegm, bsums, besi, bes, bden, brr = b_sm2[pr]
            dve.op(lambda e: e.tensor_tensor(out=den, in0=sums, in1=es, op=ALU.add), reads=[bsums, bes], writes=[bden])
            dve.op(lambda e: e.reciprocal(out=rr, in_=den), reads=[bden], writes=[brr])
            for h in range(4):
                dg = diag2[:, pr * 4 + h, :]
                act.op(lambda e, h=h, dg=dg: e.activation(out=dg, in_=identb, func=AF.Identity, scale=rr[:, h:h + 1]),
                       reads=[brr, b_const], writes=[b_diag2[pr * 4 + h]])

        def ptpv(ui):
            qb, g = units[ui]
            pr = ui % 2
            dl, nkb, mo, kcol0 = geom(qb)
            pvp = pvs[pr]
            for kb in range(nkb):
                pb = 4 + (kb % 2)
                for h in range(4):
                    P.mm(bank[pb][:, h * 128:(h + 1) * 128], b_bank[pb], pvp[:, h, kb * 128:(kb + 1) * 128], diag2[:, pr * 4 + h, :],
                         reads=[b_ps[pr], b_diag2[pr * 4 + h]], start=True, stop=True, signal=(h == 3))
                copy_op(dve, ptv[:, kb, :], bank[pb], [b_bank[pb]], [b_pt[kb]])
            items = []
            for kb in range(nkb):
                blk = qb + dl[kb] + 1
                items.append((vv[:, blk, g * 128:(g + 1) * 128], ptv[:, kb, :]))
            P.mm_group(bank[7], b_bank[7], items, reads=[b_v] + b_pt[:nkb])
            o = aoT[:, g * 4:(g + 1) * 4, qb * 128:(qb + 1) * 128]
            i_ = bank[7].rearrange("p (h q) -> p h q", h=4)
            dve.op(lambda e: e.tensor_copy(out=o, in_=i_), reads=[b_bank[7]], writes=b_ao[g * 4:(g + 1) * 4])
            s = g % 2
            pool.op(lambda e: e.tensor_tensor(out=sqs[s].rearrange("p (h q) -> p h q", h=4), in0=o, in1=o, op=ALU.mult),
                    reads=b_ao[g * 4:(g + 1) * 4], writes=[b_sq[s]])
            for h in range(4):
                P.mm(bank[6][:, qb * 128:(qb + 1) * 128], b_bank[6], onesb, sqs[s][:, h * 128:(h + 1) * 128],
                     reads=[b_sq[s], b_const], start=(g == 0 and h == 0), stop=(g == 1 and h == 3), signal=(h == 3),
                     skip_group_check=True)

        for ui in range(len(units)):
            scores_softmax(ui)
            if ui >= 1:
                ptpv(ui - 1)
            softmax_tail(ui)
        if units:
            ptpv(len(units) - 1)
        if KM < 3:
            return
        finish_rstd(1024)
        bgr = rows["bg"] + l * 2
        for hd in range(8):
            s = hd % 4
            eng = dve if hd % 3 != 2 else pool
            eng.op(lambda e, hd=hd, s=s: e.tensor_tensor(out=tmps[s], in0=aoT[:, hd, :], in1=rstd_t, op=ALU.mult),
                   reads=[b_ao[hd], b_rstd], writes=[b_tmp[s]])
            act.op(lambda e, hd=hd, s=s: e.activation(out=hT[:, hd, :], in_=tmps[s], func=AF.Identity, scale=vecT[:, hd, bgr:bgr + 1]),
                   reads=[b_tmp[s], b_const], writes=[b_hT[hd]])
        stats_rstd([(ftv[:, g, :], [b_ft]) for g in range(8)], 1024)
        for g in range(8):
            s = g % 4
            eng = dve if g % 3 != 2 else pool
            eng.op(lambda e, g=g, s=s: e.tensor_tensor(out=tmps[s], in0=ftv[:, g, :], in1=rstd_t, op=ALU.mult),
                   reads=[b_ft, b_rstd], writes=[b_tmp[s]])
            act.op(lambda e, g=g, s=s: e.activation(out=hT[:, 8 + g, :], in_=tmps[s], func=AF.Identity, scale=vecT[:, g, bgr + 1:bgr + 2]),
                   reads=[b_tmp[s], b_const], writes=[b_hT[8 + g]])
        if KM < 4:
            return
        for i in range(KC):
            w, bw = wload(Wo[l, i], lambda a: a[:, 0:KC * 128].rearrange("p (k c) -> p k c", k=KC), ("out", l))
            ob = 4 + (i % 2)
            P.mm_group(bank[ob], b_bank[ob], [(w[:, cc, :], hT[:, cc, :]) for cc in range(KC)], reads=[bw], item_reads=b_hT)
            evac_y(i, ob, l, 1)
        postnorm(l, 1)

    def load_x_dma(t):
        t0 = t * T
        sp.dma(xtok, x_d[t0:t0 + T, :].rearrange("(b p) d -> p b d", p=128), writes=[b_xtok] + b_yT, sembuf=b_xtok)

    def load_x_tokmajor(t):
        for c in range(KC):
            pb = c % 4
            for tb in range(4):
                o = bank[pb][:, tb * 128:(tb + 1) * 128]
                i_ = xtok[:, tb, c * 128:(c + 1) * 128]
                pe.op(lambda e, o=o, i_=i_: e.transpose(o, i_, ident32), reads=[b_xtok, b_const], writes=[b_bank[pb]], signal=(tb == 3))
            copy_op(evac_engine(), xT[:, c, :], bank[pb], [b_bank[pb]], [b_xT[c]])

    def load_xT(t):
        t0 = t * T
        sp.dma(xT, XS[:, :, t0:t0 + T].rearrange("c p t -> p c t"), writes=b_xT, sembuf=b_xT[0])

    def store_y(t):
        t0 = t * T
        for tb in range(4):
            for cg in range(4):
                pb = (tb * 4 + cg) % 4
                for k in range(4):
                    c = cg * 4 + k
                    o = bank[pb][:, k * 128:(k + 1) * 128]
                    i_ = xT[:, c, tb * 128:(tb + 1) * 128]
                    pe.op(lambda e, o=o, i_=i_: e.transpose(o, i_, ident32), reads=[b_xT[c], b_const], writes=[b_bank[pb]], signal=(k == 3))
                copy_op(evac_engine(), xtok[:, tb, cg * 512:(cg + 1) * 512], bank[pb], [b_bank[pb]], [b_xtok] + b_yT)
        if int(os.environ.get("KY", "1")):
            sp.dma(y_d[t0:t0 + T, :].rearrange("(b p) d -> p b d", p=128), xtok, reads=[b_xtok], sembuf=b_xtok)

    def fourier(l):
        par = l % 2
        o_ = [0]

        def fc(nbytes):
            a = arena[:, o_[0] // 2:(o_[0] + nbytes) // 2]
            o_[0] += nbytes
            return a
        uTs = [fc(S * 2) for _ in range(2)]
        FTs = [fc(S * 2) for _ in range(2)]
        data1 = fc(2 * 64 * 128 * 2)[0:NA].rearrange("p (b r e) -> p r e b", r=2, e=64)
        data2 = fc(2 * NA * 128 * 2).rearrange("p (e r a) -> p r a e", r=2, a=NA)
        tcp = fc(NA * 128 * 2).rearrange("p (a b) -> p a b", a=NA)
        tsp = fc(NA * 128 * 2).rearrange("p (a b) -> p a b", a=NA)
        wl = [fc(512).bitcast(F32) for _ in range(2)]
        Gb = [fc(512).rearrange("p (r e) -> p r e", r=2) for _ in range(2)]
        assert o_[0] <= main_bytes, (o_[0], main_bytes)
        b_u = [getbuf("fu%d" % i) for i in range(2)]
        b_FT = [getbuf("fF%d" % i) for i in range(2)]
        b_d1, b_d2, b_tt = getbuf("fd1"), getbuf("fd2"), getbuf("ftt")
        b_wl = [getbuf("fwl%d" % i) for i in range(2)]
        b_G = [getbuf("fG%d" % i) for i in range(2)]
        pool.dma(tcp, tcp_d.rearrange("p (a b) -> p a b", a=NA), writes=[b_tt], sembuf=b_tt, max_dma_last_dim=4096)
        pool.dma(tsp, tsp_d.rearrange("p (a b) -> p a b", a=NA), writes=[b_tt], sembuf=b_d1, max_dma_last_dim=4096)
        pbk = [0]

        def nb():
            pbk[0] = (pbk[0] + 1) % 6
            return pbk[0]
        for g in range(NG):
            s = g % 2
            sp.dma(wl[s], fw_d[l, g], writes=[b_wl[s]], sembuf=b_wl[s])
            sp.dma(uTs[s], US[par, g], writes=[b_u[s]], sembuf=b_u[s])
            gbk = 6 + s
            P.mm(bank[gbk][:, 0:128], b_bank[gbk], ccs, wl[s], reads=[b_wl[s], b_const], start=True, stop=True, signal=False)
            P.mm(bank[gbk][:, 128:256], b_bank[gbk], scs, wl[s], reads=[b_wl[s], b_const], start=True, stop=True)
            o = Gb[s]
            i_ = bank[gbk][:, 0:256].rearrange("p (r e) -> p r e", r=2)
            dve.op(lambda e, o=o, i_=i_: e.tensor_copy(out=o, in_=i_), reads=[b_bank[gbk]], writes=[b_G[s]])
            uT = uTs[s]
            KF = int(os.environ.get("KF", "9"))
            for eh in range(2 if KF >= 2 else 0):
                for b0 in range(0, 128, 4):
                    pb = nb()
                    for bi in range(4):
                        b = b0 + bi
                        lhs = uT[:, b::128]
                        rhs = Gb[s][:, :, eh * 64:(eh + 1) * 64]
                        P.mm(bank[pb][0:NA, bi * 128:(bi + 1) * 128], b_bank[pb], lhs, rhs, reads=[b_u[s], b_G[s]],
                             start=True, stop=True, signal=(bi == 3))
                    o = data1[:, :, :, b0:b0 + 4].rearrange("p r e b -> p b (r e)")
                    i_ = bank[pb][0:NA, :].rearrange("p (b x) -> p b x", b=4)
                    copy_op(evac_engine(), o, i_, [b_bank[pb]], [b_d1])
                for e0 in range(0, 64 if KF >= 3 else 0, 2):
                    pb = nb()
                    ne = min(512 // (2 * NA), 2) if 2 * NA * 2 <= 512 else 1
                    for ei in range(2):
                        e_ = e0 + ei
                        oo = bank[pb][:, ei * 2 * NA:(ei + 1) * 2 * NA]
                        P.mm(oo, b_bank[pb], data1[:, 0, e_, :], r1, reads=[b_d1, b_const], start=True, stop=False)
                        P.mm(oo, b_bank[pb], data1[:, 1, e_, :], r2, reads=[b_d1, b_const], start=False, stop=True, signal=(ei == 1))
                    ee = eh * 64 + e0
                    o = data2[:, :, :, ee:ee + 2].rearrange("p r a e -> p e (r a)")
                    i_ = bank[pb][:, 0:2 * 2 * NA].rearrange("p (e x) -> p e x", e=2)
                    copy_op(evac_engine(), o, i_, [b_bank[pb]], [b_d2])
            FT = FTs[s].rearrange("p (b a) -> p a b", a=NA)
            for a0 in range(0, NA if KF >= 4 else 0, 4):
                pb = nb()
                na = min(4, NA - a0)
                for ai in range(na):
                    a_ = a0 + ai
                    oo = bank[pb][:, ai * 128:(ai + 1) * 128]
                    P.mm(oo, b_bank[pb], data2[:, 0, a_, :], tcp[:, a_, :], reads=[b_d2, b_tt], start=True, stop=False)
                    P.mm(oo, b_bank[pb], data2[:, 1, a_, :], tsp[:, a_, :], reads=[b_d2, b_tt], start=False, stop=True, signal=(ai == na - 1))
                o = FT[:, a0:a0 + na, :]
                i_ = bank[pb][:, 0:na * 128].rearrange("p (a b) -> p a b", a=na)
                copy_op(evac_engine(), o, i_, [b_bank[pb]], [b_FT[s]])
            sp.dma(FS[par, g], FTs[s], reads=[b_FT[s]], sembuf=b_FT[s])

    STOP = int(os.environ.get("KSTOP", "99"))
    marks = {}
    MARKS.clear()
    MARKS.update({"_": marks})
    rows = prologue()
    marks["prologue_end"] = pe.cnt
    P.barrier()
    pool_free[0] = False
    if STOP >= 3:
        load_x_dma(0)
    for t in range(NT if STOP >= 3 else 0):
        load_x_tokmajor(t)
        if STOP >= 4:
            ffn(0, 0, 0)
        nxtA = (lambda t=t: load_x_dma(t + 1)) if t + 1 < NT else None
        if STOP >= 5:
            proj(0, t, nxtA)
        elif nxtA:
            nxtA()
        marks["A_t%d" % t] = pe.cnt
        P.barrier()
    pool_free[0] = True
    for l in range(L if STOP >= 6 else 0):
        fourier(l)
        marks["F%d" % l] = pe.cnt
        P.barrier()
        if STOP < 7:
            break
        load_xT(0)
        for t in range(NT):
            mixer(l, t)
            marks["M%d_t%d" % (l, t)] = pe.cnt
            P.barrier()
            if STOP < 8:
                continue
            KB = int(os.environ.get("KB", "9"))
            ffn(l, 1, 2)
            if l + 1 < L:
                nxtB = (lambda t=t: load_xT(t + 1)) if t + 1 < NT else None
                if KB >= 2:
                    ffn(l + 1, 0, 0)
                    proj(l + 1, t, nxtB)
                elif nxtB:
                    nxtB()
            else:
                if KB >= 3:
                    store_y(t)
                if t + 1 < NT:
                    load_xT(t + 1)
            marks["B%d_t%d" % (l, t)] = pe.cnt
            P.barrier()
    P.finalize()
    return nc


_CACHE = {}
MARKS = {}


def run_cores(cfg, per_core_inputs, n_cores):
    key = (cfg.S, cfg.DFF, cfg.L)
    if key not in _CACHE:
        _CACHE[key] = build(cfg)
    nc = _CACHE[key]
    consts = host_constants(cfg)
    in_maps = []
    for ci in range(n_cores):
        m = dict(per_core_inputs[ci])
        m.update(consts)
        in_maps.append(m)
    res = run_bass_kernel_spmd(nc, in_maps, core_ids=list(range(n_cores)))
    if int(os.environ.get("KDBG", "0")):
        return res.results
    return [r["y"] for r in res.results]


def kernel(x_prompt, x_sample, c_prompt, c_sample, w_mod, b_mod, pre_g, post_g,
           ffn_w_gate, ffn_w_up, ffn_w_down, w_in, attn_sink, fourier_w, branch_g, w_out):
    f = lambda a: np.ascontiguousarray(np.asarray(a, dtype=np.float32))
    xs = [f(x_prompt)[0]] + [f(x_sample)[i] for i in range(x_sample.shape[0])]
    cs = [f(c_prompt)[0:1]] + [f(c_sample)[i:i + 1] for i in range(c_sample.shape[0])]
    S = xs[0].shape[0]
    cfg = Cfg(S=S, DFF=ffn_w_gate.shape[-1], L=w_mod.shape[0])
    shared = {"w_mod": f(w_mod), "b_mod": f(b_mod), "pre_g": f(pre_g), "post_g": f(post_g),
              "ffn_w_gate": f(ffn_w_gate), "ffn_w_up": f(ffn_w_up), "ffn_w_down": f(ffn_w_down),
              "w_in": f(w_in), "attn_sink": f(attn_sink), "fourier_w": f(fourier_w),
              "branch_g": f(branch_g), "w_out": f(w_out)}
    n_cores = 8
    seq_core = [0, 1, 2, 4, 5]
    zero = {k: np.zeros_like(v) for k, v in shared.items()}
    zero["x"] = np.zeros_like(xs[0])
    zero["c"] = np.zeros_like(cs[0])
    per_core = [zero] * n_cores
    for si, ci in enumerate(seq_core):
        m = dict(shared)
        m["x"] = xs[si]
        m["c"] = cs[si]
        per_core[ci] = m
    ys = run_cores(cfg, per_core, n_cores)
    y_prompt = ys[seq_core[0]][None].astype(np.float32)
    y_sample = np.stack([ys[ci] for ci in seq_core[1:5]], axis=0).astype(np.float32)
    return (y_prompt, y_sample)
```
